# Optimizing a Trainium2 kernel written in Bass

```python
import math
import jax
import jax.numpy as jnp
from jax import lax
import numpy as np

D_MODEL = 1024
BATCH = 4
SEQ = 8192
DEPTH = 2

HEAD_DIM = 64
ATT_HEADS = (3 * D_MODEL) // (8 * HEAD_DIM)
ATT_KV_HEADS = ATT_HEADS // 3
ATT_WIDTH = ATT_HEADS * HEAD_DIM
ATT_KV_WIDTH = ATT_KV_HEADS * HEAD_DIM
WINDOW = 128
BLOCK = 128
ROPE_DIM = HEAD_DIM // 4
ROPE_THETA = 500000.0
DN_HEADS = (3 * D_MODEL) // (8 * HEAD_DIM)
DN_WIDTH = DN_HEADS * HEAD_DIM
DN_CONV = 5
DN_CHUNK = 64
RW_HEADS = (D_MODEL - ATT_WIDTH - DN_WIDTH) // HEAD_DIM
RW_WIDTH = RW_HEADS * HEAD_DIM
W_RANK = 64
A_RANK = 64
G_RANK = 128
RW_IN = 3 * RW_WIDTH + W_RANK + A_RANK + G_RANK
D_FF = (11 * D_MODEL) // 4
FFN_CONV = 3
NORM_EPS = 1e-6
RW_LN_EPS = 64e-5
IN_SPLITS = (ATT_WIDTH, ATT_KV_WIDTH, ATT_KV_WIDTH, 3 * DN_WIDTH, DN_WIDTH,
             2 * DN_HEADS, 2 * DN_HEADS, RW_IN)
N_IN = sum(IN_SPLITS)

kernel_name = 'hybrid_parallel_head_encoder'


def _split(t, sizes):
    idx = [int(i) for i in np.cumsum(sizes)[:-1]]
    return jnp.split(t, idx, axis=-1)


def rms_norm(x, g):
    xf = x.astype(jnp.float32)
    y = xf * lax.rsqrt(jnp.mean(xf * xf, -1, keepdims=True) + NORM_EPS)
    return (y * g.astype(jnp.float32)).astype(x.dtype)


def l2norm(t):
    return t * lax.rsqrt(jnp.sum(t * t, -1, keepdims=True) + 1e-6)


def depthwise_conv(x, w):
    pad = (w.shape[0] - 1) // 2
    return lax.conv_general_dilated(
        x, w[:, None, :].astype(x.dtype), window_strides=(1,), padding=[(pad, pad)],
        dimension_numbers=('NWC', 'WIO', 'NWC'), feature_group_count=x.shape[-1])


def partial_rotary(t, positions):
    half = ROPE_DIM // 2
    inv_freq = ROPE_THETA ** (-jnp.arange(half, dtype=jnp.float32) / half)
    ang = positions.astype(jnp.float32)[:, :, None, None] * inv_freq
    cos, sin = jnp.cos(ang), jnp.sin(ang)
    tf = t.astype(jnp.float32)
    t1, t2, rest = tf[..., :half], tf[..., half:ROPE_DIM], tf[..., ROPE_DIM:]
    out = jnp.concatenate([t1 * cos - t2 * sin, t2 * cos + t1 * sin, rest], -1)
    return out.astype(t.dtype)


def banded_window_attention(q, k, v, sink):
    B, S, _, _ = q.shape
    nb = S // BLOCK
    G = ATT_HEADS // ATT_KV_HEADS
    qb = q.reshape(B, nb, BLOCK, ATT_KV_HEADS, G, HEAD_DIM)

    def windows(t):
        tb = t.reshape(B, nb, BLOCK, ATT_KV_HEADS, HEAD_DIM)
        tp = jnp.pad(tb, ((0, 0), (1, 1), (0, 0), (0, 0), (0, 0)))
        return jnp.concatenate([tp[:, :-2], tp[:, 1:-1], tp[:, 2:]], axis=2)

    kw, vw = windows(k), windows(v)
    s = jnp.einsum('bnqhgd,bnkhd->bnhgqk', qb, kw,
                   preferred_element_type=jnp.float32) * (HEAD_DIM ** -0.5)
    qpos = jnp.arange(BLOCK)[:, None]
    kpos = jnp.arange(3 * BLOCK)[None, :] - BLOCK
    abs_k = jnp.arange(nb)[:, None, None] * BLOCK + kpos
    valid = (jnp.abs(kpos - qpos) <= WINDOW)[None] & (abs_k >= 0) & (abs_k < S)
    s = jnp.where(valid[None, :, None, None], s, -jnp.inf)
    sk = sink.astype(jnp.float32).reshape(1, 1, ATT_KV_HEADS, G, 1, 1)
    m = jnp.maximum(jnp.max(s, -1, keepdims=True), sk)
    p = jnp.exp(s - m)
    p = p / (jnp.sum(p, -1, keepdims=True) + jnp.exp(sk - m))
    o = jnp.einsum('bnhgqk,bnkhd->bnqhgd', p.astype(v.dtype), vw)
    return o.reshape(B, S, ATT_WIDTH)


def attention_mixer(q, k, v, positions, sink):
    B, S, _ = q.shape
    q = partial_rotary(q.reshape(B, S, ATT_HEADS, HEAD_DIM), positions)
    k = partial_rotary(k.reshape(B, S, ATT_KV_HEADS, HEAD_DIM), positions)
    v = v.reshape(B, S, ATT_KV_HEADS, HEAD_DIM)
    return banded_window_attention(q, k, v, sink)


def gated_delta_chunked(q, k, v, g, beta):
    B, H, S, Dk = q.shape
    Dv = v.shape[-1]
    C = DN_CHUNK
    N = S // C
    q = q * (Dk ** -0.5)
    q, k, v = [t.reshape(B, H, N, C, t.shape[-1]) for t in (q, k, v)]
    g, beta = g.reshape(B, H, N, C), beta.reshape(B, H, N, C)
    gc = jnp.cumsum(g, axis=-1)
    tril = jnp.tril(jnp.ones((C, C), bool))
    strict = jnp.tril(jnp.ones((C, C), bool), -1)
    decay = jnp.exp(jnp.where(tril, gc[..., :, None] - gc[..., None, :], -jnp.inf))
    k_beta = k * beta[..., None]
    lower = jnp.where(strict, jnp.einsum('bhnid,bhnjd->bhnij', k_beta, k) * decay, 0.0)
    a_mat = lower + jnp.eye(C, dtype=lower.dtype)
    rhs = jnp.concatenate([v * beta[..., None], k_beta * jnp.exp(gc)[..., None]], -1)
    sol = lax.linalg.triangular_solve(a_mat, rhs, left_side=True, lower=True, unit_diagonal=True)
    value, k_cum = sol[..., :Dv], sol[..., Dv:]
    qk = jnp.einsum('bhnid,bhnjd->bhnij', q, k) * decay
    q_g = q * jnp.exp(gc)[..., None]
    k_g = k * jnp.exp(gc[..., -1:] - gc)[..., None]
    g_last = jnp.exp(gc[..., -1])

    def step(state, inp):
        value_i, kcum_i, qk_i, qg_i, kg_i, gl_i = inp
        v_new = value_i - jnp.einsum('bhcd,bhde->bhce', kcum_i, state)
        o = jnp.einsum('bhcd,bhde->bhce', qg_i, state) + jnp.einsum('bhij,bhje->bhie', qk_i, v_new)
        state = state * gl_i[..., None, None] + jnp.einsum('bhcd,bhce->bhde', kg_i, v_new)
        return state, o

    xs = tuple(jnp.moveaxis(t, 2, 0) for t in (value, k_cum, qk, q_g, k_g, g_last))
    _, o = lax.scan(step, jnp.zeros((B, H, Dk, Dv), jnp.float32), xs)
    return jnp.moveaxis(o, 0, 2).reshape(B, H, S, Dv)


def deltanet_mixer(qkv, z, alpha, beta_logit, conv_w, a_log, dt_bias, norm_w):
    B, S, _ = qkv.shape
    f32 = jnp.float32
    qkv = jax.nn.silu(depthwise_conv(qkv, conv_w)).astype(f32)
    q, k, v = [t.reshape(B, S, DN_HEADS, HEAD_DIM).transpose(0, 2, 1, 3)
               for t in jnp.split(qkv, 3, axis=-1)]
    q, k = l2norm(q), l2norm(k)
    alpha = alpha.astype(f32).reshape(B, S, 2, DN_HEADS)
    g = -jnp.exp(a_log.astype(f32)) * jax.nn.softplus(alpha + dt_bias.astype(f32))
    beta = jax.nn.sigmoid(beta_logit.astype(f32).reshape(B, S, 2, DN_HEADS))
    g, beta = g.transpose(2, 0, 3, 1), beta.transpose(2, 0, 3, 1)
    o_fwd = gated_delta_chunked(q, k, v, g[0], beta[0])
    rev = lambda t: jnp.flip(t, axis=2)
    o_bwd = rev(gated_delta_chunked(rev(q), rev(k), rev(v), rev(g[1]), rev(beta[1])))
    o = (o_fwd + o_bwd).transpose(0, 2, 1, 3)
    o = o * lax.rsqrt(jnp.mean(o * o, -1, keepdims=True) + NORM_EPS) * norm_w.astype(f32)
    o = o * jax.nn.silu(z.astype(f32).reshape(B, S, DN_HEADS, HEAD_DIM))
    return o.reshape(B, S, DN_WIDTH).astype(z.dtype)


def rwkv7_scan(r, w, k, v, a, b):
    B, S, H, N = r.shape
    xs = tuple(jnp.moveaxis(t, 1, 0) for t in (r, w, k, v, a, b))

    def step(state, inp):
        r_t, w_t, k_t, v_t, a_t, b_t = inp
        sa = jnp.einsum('bhvk,bhk->bhv', state, a_t)
        state = (state * w_t[:, :, None, :] + sa[..., None] * b_t[:, :, None, :]
                 + v_t[..., None] * k_t[:, :, None, :])
        return state, jnp.einsum('bhvk,bhk->bhv', state, r_t)

    _, y = lax.scan(step, jnp.zeros((B, H, N, N), jnp.float32), xs)
    return jnp.moveaxis(y, 0, 1)


def rwkv_mixer(p, mu, w0, w_up, a0, a_up, g_up, k_k, k_a, r_k, ln_w, ln_b):
    B, S, _ = p.shape
    out_dtype = p.dtype
    f32 = jnp.float32
    p = p.astype(f32)
    prev = jnp.pad(p, ((0, 0), (1, 0), (0, 0)))[:, :-1]
    nxt = jnp.pad(p, ((0, 0), (0, 1), (0, 0)))[:, 1:]
    p = p + mu[0].astype(f32) * (prev - p) + mu[1].astype(f32) * (nxt - p)
    r, k, v, wd, ad, gd = _split(p, (RW_WIDTH, RW_WIDTH, RW_WIDTH, W_RANK, A_RANK, G_RANK))
    wl = jnp.tanh(wd)
    g = jax.nn.sigmoid(gd) @ g_up.astype(f32)
    heads = lambda t: t.reshape(B, S, RW_HEADS, HEAD_DIM)
    kk = l2norm(heads(k * k_k.astype(f32)))

    def direction(d):
        w = -jax.nn.softplus(-(w0[d].astype(f32) + wl @ w_up[d].astype(f32))) - 0.5
        decay = jnp.exp(-jnp.exp(w))
        a = jax.nn.sigmoid(a0[d].astype(f32) + ad @ a_up[d].astype(f32))
        kd = k * (1.0 + (a - 1.0) * k_a.astype(f32))
        return heads(decay), heads(kd), heads(a)

    rh, vh = heads(r), heads(v)
    dec_f, k_f, a_f = direction(0)
    y_f = rwkv7_scan(rh, dec_f, k_f, vh, -kk, kk * a_f)
    dec_b, k_b, a_b = direction(1)
    rev = lambda t: jnp.flip(t, axis=1)
    y_b = rev(rwkv7_scan(rev(rh), rev(dec_b), rev(k_b), rev(vh), rev(-kk), rev(kk * a_b)))
    y = y_f + y_b
    mean = jnp.mean(y, -1, keepdims=True)
    var = jnp.mean(jnp.square(y - mean), -1, keepdims=True)
    y = ((y - mean) * lax.rsqrt(var + RW_LN_EPS)).reshape(B, S, RW_WIDTH) * ln_w.astype(f32) + ln_b.astype(f32)
    bonus = jnp.sum(rh * heads(k) * r_k.astype(f32), -1, keepdims=True) * vh
    return ((y + bonus.reshape(B, S, RW_WIDTH)) * g).astype(out_dtype)


def conv_ffn(h, w_up, conv_w, w_down):
    u = depthwise_conv(h @ w_up, conv_w)
    gate, val = jnp.split(u, 2, axis=-1)
    return (jax.nn.silu(gate) * val) @ w_down


def setup_inputs(seed: int = 0) -> dict:
    key = jax.random.key(seed)
    ks = jax.random.split(key, 32)
    f32 = jnp.float32
    L = DEPTH

    def nrm(i, shape, scale):
        return scale * jax.random.normal(ks[i], shape, f32)

    def uni(i, shape, lo, hi):
        return jax.random.uniform(ks[i], shape, f32, lo, hi)

    x = nrm(0, (BATCH, SEQ, D_MODEL), 1.0)
    positions = (jnp.arange(SEQ, dtype=jnp.int32)[None, :]
                 + jax.random.randint(ks[1], (BATCH, 1), 0, SEQ, dtype=jnp.int32))
    norm_mix = 1.0 + nrm(2, (L, D_MODEL), 0.02)
    w_in = nrm(3, (L, D_MODEL, N_IN), D_MODEL ** -0.5)
    attn_sink = nrm(4, (L, ATT_HEADS), 0.5)
    dn_conv = nrm(5, (L, DN_CONV, 3 * DN_WIDTH), DN_CONV ** -0.5)
    dn_a_log = jnp.log(uni(6, (L, 2, DN_HEADS), 1.0, 16.0))
    dt = jnp.exp(uni(7, (L, 2, DN_HEADS), math.log(1e-3), math.log(1e-1)))
    dn_dt_bias = dt + jnp.log(-jnp.expm1(-dt))
    dn_norm = 1.0 + nrm(8, (L, HEAD_DIM), 0.02)
    rw_mu = uni(9, (L, 2, RW_IN), 0.0, 0.5)
    rw_w0 = uni(10, (L, 2, RW_WIDTH), -2.0, 1.0)
    rw_w_up = nrm(11, (L, 2, W_RANK, RW_WIDTH), 0.5 * W_RANK ** -0.5)
    rw_a0 = nrm(12, (L, 2, RW_WIDTH), 0.1)
    rw_a_up = nrm(13, (L, 2, A_RANK, RW_WIDTH), 0.5 * A_RANK ** -0.5)
    rw_g_up = nrm(14, (L, G_RANK, RW_WIDTH), G_RANK ** -0.5)
    rw_k_k = 0.85 + nrm(15, (L, RW_WIDTH), 0.02)
    rw_k_a = 1.0 + nrm(16, (L, RW_WIDTH), 0.02)
    rw_r_k = nrm(17, (L, RW_HEADS, HEAD_DIM), 0.1)
    rw_ln_w = 1.0 + nrm(18, (L, RW_WIDTH), 0.02)
    rw_ln_b = nrm(19, (L, RW_WIDTH), 0.01)
    w_out = nrm(20, (L, D_MODEL, D_MODEL), D_MODEL ** -0.5)
    norm_ffn = 1.0 + nrm(21, (L, D_MODEL), 0.02)
    ffn_w_up = nrm(22, (L, D_MODEL, 2 * D_FF), D_MODEL ** -0.5)
    ffn_conv = nrm(23, (L, FFN_CONV, 2 * D_FF), FFN_CONV ** -0.5)
    ffn_w_down = nrm(24, (L, D_FF, D_MODEL), D_FF ** -0.5)
    norm_final = 1.0 + nrm(25, (D_MODEL,), 0.02)
    return {'x': x, 'positions': positions, 'norm_mix': norm_mix, 'w_in': w_in,
            'attn_sink': attn_sink, 'dn_conv': dn_conv, 'dn_a_log': dn_a_log,
            'dn_dt_bias': dn_dt_bias, 'dn_norm': dn_norm, 'rw_mu': rw_mu, 'rw_w0': rw_w0,
            'rw_w_up': rw_w_up, 'rw_a0': rw_a0, 'rw_a_up': rw_a_up, 'rw_g_up': rw_g_up,
            'rw_k_k': rw_k_k, 'rw_k_a': rw_k_a, 'rw_r_k': rw_r_k, 'rw_ln_w': rw_ln_w,
            'rw_ln_b': rw_ln_b, 'w_out': w_out, 'norm_ffn': norm_ffn, 'ffn_w_up': ffn_w_up,
            'ffn_conv': ffn_conv, 'ffn_w_down': ffn_w_down, 'norm_final': norm_final}


def reference(x, positions, norm_mix, w_in, attn_sink, dn_conv, dn_a_log, dn_dt_bias, dn_norm,
              rw_mu, rw_w0, rw_w_up, rw_a0, rw_a_up, rw_g_up, rw_k_k, rw_k_a, rw_r_k,
              rw_ln_w, rw_ln_b, w_out, norm_ffn, ffn_w_up, ffn_conv, ffn_w_down, norm_final):
    for l in range(DEPTH):
        h = rms_norm(x, norm_mix[l])
        proj = h @ w_in[l]
        aq, ak, av, dqkv, dz, dalpha, dbeta, rwp = _split(proj, IN_SPLITS)
        y_att = attention_mixer(aq, ak, av, positions, attn_sink[l])
        y_dn = deltanet_mixer(dqkv, dz, dalpha, dbeta, dn_conv[l], dn_a_log[l],
                              dn_dt_bias[l], dn_norm[l])
        y_rw = rwkv_mixer(rwp, rw_mu[l], rw_w0[l], rw_w_up[l], rw_a0[l], rw_a_up[l],
                          rw_g_up[l], rw_k_k[l], rw_k_a[l], rw_r_k[l],
                          rw_ln_w[l], rw_ln_b[l])
        mixed = jnp.concatenate([y_att.astype(x.dtype), y_dn.astype(x.dtype),
                                 y_rw.astype(x.dtype)], axis=-1)
        x = x + mixed @ w_out[l]
        h = rms_norm(x, norm_ffn[l])
        x = x + conv_ffn(h, ffn_w_up[l], ffn_conv[l], ffn_w_down[l])
    return rms_norm(x, norm_final)
```

```python
import contextlib
import numpy as np
import concourse.bass as bass
import concourse.mybir as mybir
from concourse.bass_utils import run_bass_kernel_spmd

F32 = mybir.dt.float32
BF16 = mybir.dt.bfloat16
AF = mybir.ActivationFunctionType
ALU = mybir.AluOpType
AX = mybir.AxisListType

ENGS = ("pe", "dve", "act", "pool", "sp")

D = 1024
HD = 64
NQ = 6
DFF = 2816
N_IN = 3224
NTM = 2712
C_AV, C_DQKV, C_DZ, C_DAL, C_DBE, C_RW = 0, 128, 1280, 1664, 1676, 1688
EPS = 1e-6


class Buf:
    __slots__ = ("name", "w", "r")

    def __init__(self, name=""):
        self.name = name
        self.w = None
        self.r = {}


class Prog:
    NDMA = 24

    def __init__(self, nc):
        self.nc = nc
        self.q = {e: [] for e in ENGS}
        self.cnt = {e: 0 for e in ENGS}
        self.seen = {e: {} for e in ENGS}
        self.ndma = 0
        self.sems = {}
        self.n_instr = 0

    def _need(self, eng, tok, waits):
        if tok is None:
            return
        k, v = tok
        if self.seen[eng].get(k, 0) >= v:
            return
        if waits.get(k, 0) < v:
            waits[k] = v

    def _deps(self, eng, reads, writes):
        waits = {}
        for b in reads:
            self._need(eng, b.w, waits)
        for b in writes:
            self._need(eng, b.w, waits)
            for k, v in b.r.items():
                self._need(eng, (k, v), waits)
        for k, v in waits.items():
            self.seen[eng][k] = v
        return list(waits.items())

    def _mark(self, tok, reads, writes):
        k, v = tok
        for b in reads:
            if b.r.get(k, 0) < v:
                b.r[k] = v
        for b in writes:
            b.w = tok
            b.r = {}

    def op(self, eng, fn, reads=(), writes=(), inc=True):
        waits = self._deps(eng, reads, writes)
        if inc:
            self.cnt[eng] += 1
            tok = (eng, self.cnt[eng])
        else:
            tok = (eng, self.cnt[eng] + 1)
        self.q[eng].append((waits, fn, (eng, 1) if inc else None))
        self._mark(tok, reads, writes)
        self.n_instr += 1
        return tok

    def dma(self, eng, out, in_, reads=(), writes=(), slow=False):
        i = self.ndma
        self.ndma += 1
        key = "d%d" % (i % self.NDMA)
        val = 16 * (i // self.NDMA + 1)
        waits = dict(self._deps(eng, reads, writes))
        if val > 16 and self.seen[eng].get(key, 0) < val - 16:
            waits[key] = max(waits.get(key, 0), val - 16)
            self.seen[eng][key] = val - 16
        if slow:
            fn = lambda e, out=out, in_=in_: e.dma_start(out=out, in_=in_, allow_slow_non_contiguous=True)
        else:
            fn = lambda e, out=out, in_=in_: e.dma_start(out=out, in_=in_)
        self.q[eng].append((list(waits.items()), fn, (key, 16)))
        tok = (key, val)
        self._mark(tok, reads, writes)
        self.n_instr += 1
        return tok

    def wait_all(self, eng, bufs):
        waits = self._deps(eng, bufs, ())
        if waits:
            self.q[eng].append((waits, None, None))

    def begin(self, st):
        keys = [e for e in ENGS if e != "sp"] + ["d%d" % i for i in range(self.NDMA)]
        for k in keys:
            self.sems[k] = st.enter_context(self.nc.semaphore("s_" + k))

    def barrier(self):
        toks = [(e, self.cnt[e]) for e in ENGS if e != "sp" and self.cnt[e] > 0]
        for i in range(min(self.ndma, self.NDMA)):
            n_on = (self.ndma - 1 - i) // self.NDMA + 1
            toks.append(("d%d" % i, 16 * n_on))
        for e in ENGS:
            waits = {}
            for t in toks:
                self._need(e, t, waits)
            for k, v in waits.items():
                self.seen[e][k] = v
            if waits:
                self.q[e].append((list(waits.items()), None, None))

    def flush(self):
        nc = self.nc
        with contextlib.ExitStack() as st:
            block = st.enter_context(nc.Block())
            sems = self.sems

            def run(e, q):
                for waits, fn, inc in q:
                    for k, v in waits:
                        e.wait_ge(sems[k], v)
                    if fn is None:
                        continue
                    ins = fn(e)
                    if inc is not None:
                        ins.then_inc(sems[inc[0]], inc[1])

            @block.tensor
            def _(e):
                run(e, self.q["pe"])

            @block.vector
            def _(e):
                run(e, self.q["dve"])

            @block.scalar
            def _(e):
                run(e, self.q["act"])

            @block.gpsimd
            def _(e):
                run(e, self.q["pool"])

            @block.sync
            def _(e):
                run(e, self.q["sp"])
        self.q = {e: [] for e in ENGS}


class Tl:
    def __init__(self, t, name):
        self.t = t
        self.b = Buf(name)

    def __getitem__(self, k):
        return self.t[k]


class K:
    def __init__(self, nc, T, L):
        self.nc, self.T, self.L = nc, T, L
        self.P = Prog(nc)
        self.st = contextlib.ExitStack()
        self.cur = self.st
        self.psn = 0
        self.rr = 0
        self.dq = 0

    def sb(self, name, shape, dt=F32):
        self.nalloc = getattr(self, "nalloc", 0) + 1
        nm = "%s_%d" % (name, self.nalloc)
        return Tl(self.cur.enter_context(self.nc.sbuf_tensor(nm, list(shape), dt)), nm)

    @contextlib.contextmanager
    def phase(self):
        old = self.cur
        with contextlib.ExitStack() as ph:
            self.cur = ph
            yield
            self.P.barrier()
        self.cur = old

    def dram(self, name, shape, dt=F32, kind="Internal"):
        t = self.nc.dram_tensor(name, list(shape), dt, kind=kind)
        tl = Tl(t.ap(), name)
        return tl

    def init_psum(self):
        self.banks = [Tl(self.st.enter_context(self.nc.psum_tensor("ps%d" % i, [128, 512], F32)), "ps%d" % i)
                      for i in range(7)]
        self.bankb = Tl(self.st.enter_context(self.nc.psum_tensor("psb", [128, 1024], BF16)), "psb")

    def ps(self):
        b = self.banks[self.psn % 7]
        self.psn += 1
        return b

    def _bw(self, R, W):
        return [t.b for t in R], [t.b for t in W]

    def mm(self, out, lhsT, rhs, R, W, start=True, stop=True, inc=None):
        r, w = self._bw(R, W)
        inc = True
        self.P.op("pe", lambda e: e.matmul(out, lhsT=lhsT, rhs=rhs, start=start, stop=stop), r, w, inc=inc)

    def tr(self, out, in_, ident, R, W):
        r, w = self._bw(R, W)
        self.P.op("pe", lambda e: e.transpose(out=out, in_=in_, identity=ident), r, w)

    def act(self, out, in_, func, R, W, bias=None, scale=None, eng="act"):
        r, w = self._bw(R, W)
        kw = {}
        if bias is not None:
            kw["bias"] = bias
        if scale is not None:
            kw["scale"] = scale
        self.P.op("act", lambda e: e.activation(out=out, in_=in_, func=func, **kw), r, w)

    def ve(self):
        self.rr += 1
        return "dve" if self.rr % 2 else "pool"

    def tt(self, out, a, b, op, R, W, eng="dve"):
        r, w = self._bw(R, W)
        self.P.op(eng, lambda e: e.tensor_tensor(out=out, in0=a, in1=b, op=op), r, w)

    def ts(self, out, a, s1, s2, op0, op1, R, W, eng="dve"):
        r, w = self._bw(R, W)
        if s2 is None:
            self.P.op(eng, lambda e: e.tensor_scalar(out=out, in0=a, scalar1=s1, scalar2=None, op0=op0), r, w)
        else:
            self.P.op(eng, lambda e: e.tensor_scalar(out=out, in0=a, scalar1=s1, scalar2=s2, op0=op0, op1=op1), r, w)

    def stt(self, out, a, s, b, op0, op1, R, W, eng="dve"):
        eng = "dve"
        r, w = self._bw(R, W)
        self.P.op(eng, lambda e: e.scalar_tensor_tensor(out=out, in0=a, scalar=s, in1=b, op0=op0, op1=op1), r, w)

    def cp(self, out, in_, R, W, eng="dve"):
        r, w = self._bw(R, W)
        if eng == "act":
            self.P.op("act", lambda e: e.activation(out=out, in_=in_, func=AF.Copy), r, w)
        else:
            self.P.op(eng, lambda e: e.tensor_copy(out=out, in_=in_), r, w)

    def recip(self, out, in_, R, W):
        r, w = self._bw(R, W)
        self.P.op("dve", lambda e: e.reciprocal(out=out, in_=in_), r, w)

    def red(self, out, in_, op, R, W, eng="dve"):
        r, w = self._bw(R, W)
        self.P.op(eng, lambda e: e.tensor_reduce(out=out, in_=in_, axis=AX.X, op=op), r, w)

    def memset(self, out, val, W, eng="dve"):
        r, w = self._bw([], W)
        self.P.op(eng, lambda e: e.memset(out, val), r, w)

    def dma(self, out, in_, R, W, eng=None, slow=False):
        if eng is None:
            eng = "sp"
        r, w = self._bw(R, W)
        return self.P.dma(eng, out, in_, r, w, slow=slow)

    def evac(self, out, in_, R, W):
        self.rr += 1
        self.cp(out, in_, R, W, eng="act" if self.rr % 2 else "dve")


CI, CO, CA, CB, CC, CD, CSC, CNSC, CM0, CM1, CMISC = 0, 128, 256, 384, 512, 640, 768, 896, 1024, 1280, 2304
NCONST = 2320
PR_SINK, PR_CONV, PR_ALOG, PR_DTB, PR_DNN = 0, 16, 5776, 5788, 5800
PR_MU0, PR_MU1, PR_W0, PR_A0, PR_KK, PR_KA, PR_RK, PR_LNW, PR_LNB, NPR = \
    6184, 7208, 8232, 8744, 9256, 9512, 9768, 10024, 10280, 10536
MAGIC = 12582912.0
TWO_PI = 6.283185307179586


def build(T, L, mixers=("att", "dn", "rw")):
    nc = bass.Bass("TRN2", target_bir_lowering=False)
    k = K(nc, T, L)
    P = k.P
    NT = T // 128
    TW = 512 if T % 512 == 0 else 256
    NW = T // TW
    SUB = TW // 128
    inp = lambda name, shape, dt=F32: k.dram(name, shape, dt, kind="ExternalInput")
    xT_in = inp("xT", [D, T])
    pos_in = inp("posf", [16, T])
    w_in_fm = inp("w_in_fm", [L, D, 1024])
    w_in_tm = inp("w_in_tm", [L, D, NTM])
    w_out = inp("w_out", [L, D, D])
    w_up = inp("w_up", [L, D, 2 * DFF])
    w_dn = inp("w_down", [L, DFF, D])
    rw_wup = inp("rw_wup", [L, 2, 64, 256])
    rw_aup = inp("rw_aup", [L, 2, 64, 256])
    rw_gup = inp("rw_gup", [L, 128, 256])
    pvec = inp("pvec", [L, 128, 256])
    prow = inp("prow", [L, 128, NPR])
    consts = inp("consts", [128, NCONST])
    amask_in = inp("amask", [3, 128, 384])
    outT = k.dram("outT", [D, T], F32, kind="ExternalOutput")
    xs = k.dram("xs", [D, T])
    ptok = k.dram("ptok", [T + 4, NTM])
    qkr = k.dram("qkr", [512, T + 256], BF16)
    mixT = k.dram("mixT", [D, T], BF16)
    hpad = k.dram("hpad", [D, T + 2], BF16)
    ropeCS = k.dram("ropeCS", [2, 128, T])
    wupb = k.dram("wupb", [D, 2 * DFF], BF16)
    ofw = k.dram("ofw", [T, 640])

    with k.st:
        P.begin(k.st)
        k.init_psum()
        cst = k.sb("cst", [128, NCONST])
        k.dma(cst[:, :], consts[:, :], [consts], [cst])
        ident = cst[:, CI:CI + 128]
        cbf = k.sb("cbf", [128, 256], BF16)
        k.cp(cbf[:, :], cst[:, 0:256], [cst], [cbf])
        identb = cbf[:, 0:128]
        onesb = cbf[:, 128:256]
        ztf = k.sb("ztf", [128, 1024])
        k.memset(ztf[:, :], 0.0, [ztf])
        ztb = ztf[:, :].bitcast(BF16)
        amask = k.sb("amask", [128, 3, 384])
        for i in range(3):
            k.dma(amask[:, i, :], amask_in[i, :, :], [amask_in], [amask])
        lnb = cst[:, CMISC + 3:CMISC + 4]

        import os
        KSTOP = int(os.environ.get("KSTOP", "99"))
        if KSTOP == -1:
            P.flush()
            return nc, k
        with k.phase():
            for c0 in range(0, NTM, 1024):
                cw = min(1024, NTM - c0)
                k.dma(ptok[0:2, c0:c0 + cw], ztf[0:2, 0:cw], [ztf], [ptok])
                k.dma(ptok[T + 2:T + 4, c0:c0 + cw], ztf[0:2, 0:cw], [ztf], [ptok])
            k.dma(qkr[384:512, 0:128], ztb[:, 0:128], [ztf], [qkr])
            k.dma(qkr[384:512, T + 128:T + 256], ztb[:, 0:128], [ztf], [qkr])
            for c in range(8):
                k.dma(hpad[c * 128:(c + 1) * 128, 0:1], ztb[:, 0:1], [ztf], [hpad], slow=True)
                k.dma(hpad[c * 128:(c + 1) * 128, T + 1:T + 2], ztb[:, 0:1], [ztf], [hpad], slow=True)
            if KSTOP == -2:
                P.flush()
                return nc, k
            RW_ = 2048 if T % 2048 == 0 else T
            ang = k.sb("ang", [128, RW_])
            rr_ = k.sb("rr", [128, RW_])
            tab = k.sb("tab", [128, RW_])
            k.memset(tab[:, :], 0.0, [tab])
            for t0 in range(0, T, RW_):
                for base in (0, 64):
                    sl = slice(base, base + 16)
                    k.dma(ang[sl, :], pos_in[:, t0:t0 + RW_], [pos_in], [ang])
                    k.ts(ang[sl, :], ang[sl, :], cst[sl, CMISC:CMISC + 1], None, ALU.mult, None, [ang, cst], [ang])
                for ti, shift, sgncol in ((0, 1.5707963267948966, CMISC + 2), (1, 0.0, CMISC + 1)):
                    for base in (0, 64):
                        sl = slice(base, base + 16)
                        k.ts(rr_[sl, :], ang[sl, :], shift, 1.0 / TWO_PI, ALU.add, ALU.mult, [ang], [rr_])
                        k.ts(rr_[sl, :], rr_[sl, :], MAGIC, None, ALU.add, None, [rr_], [rr_])
                        k.ts(rr_[sl, :], rr_[sl, :], MAGIC, None, ALU.subtract, None, [rr_], [rr_])
                        k.stt(rr_[sl, :], rr_[sl, :], -TWO_PI, ang[sl, :], ALU.mult, ALU.add, [rr_, ang], [rr_])
                        k.ts(rr_[sl, :], rr_[sl, :], shift, 3.14159, ALU.add, ALU.min, [rr_], [rr_])
                        k.ts(rr_[sl, :], rr_[sl, :], -3.14159, None, ALU.max, None, [rr_], [rr_])
                        k.act(tab[sl, :], rr_[sl, :], AF.Sin, [rr_], [tab])
                        k.ts(tab[sl, :], tab[sl, :], cst[sl, sgncol:sgncol + 1], None, ALU.mult, None, [tab, cst], [tab])
                    k.dma(ropeCS[ti, :, t0:t0 + RW_], tab[:, :], [tab], [ropeCS])

        import os
        KSTOP = int(os.environ.get("KSTOP", "99"))
        if KSTOP == 0:
            P.flush()
            return nc, k
        wsn = [0]

        def load_w(wst, dst_tl, dst_ap_fn, src_ap_fn, rows, cols):
            for r in range(rows):
                c0 = 0
                while c0 < cols:
                    cw = min(2048, cols - c0)
                    s = wst[wsn[0] % len(wst)]
                    wsn[0] += 1
                    k.dma(s[:, 0:cw], src_ap_fn(r, c0, cw), [], [s])
                    eng = ("pool", "dve", "act")[wsn[0] % 3]
                    k.cp(dst_ap_fn(r, c0, cw), s[:, 0:cw], [s], [dst_tl], eng=eng)
                    c0 += cw

        def rms_to_h(x_, hT, sq, rstd, pv, gcol, width):
            for c in range(8):
                k.tt(sq[:, c, 0:width], x_[:, c, 0:width], x_[:, c, 0:width], ALU.mult, [x_], [sq], eng="pool" if c % 2 else "dve")
            pb = k.ps()
            for c in range(8):
                k.mm(pb[:, 0:width], onesb, sq[:, c, 0:width], [cbf, sq], [pb], start=(c == 0), stop=(c == 7))
            k.act(rstd[:, 0:width], pb[:, 0:width], AF.Ln, [pb, cst], [rstd], bias=lnb, scale=1.0)
            k.act(rstd[:, 0:width], rstd[:, 0:width], AF.Exp, [rstd], [rstd], scale=-0.5)
            for c in range(8):
                k.stt(hT[:, c, 0:width], x_[:, c, 0:width], pv[:, gcol + c:gcol + c + 1], rstd[:, 0:width], ALU.mult, ALU.mult,
                      [x_, pv, rstd], [hT], eng="pool" if c % 2 else "dve")

        def invert(E, ET, PN, PT):
            k.tt(PN[0][:, 128:256], identb, E[:, :], ALU.subtract, [cbf, E], [PN[0]])
            pb = k.ps()
            k.mm(pb[:, 0:128], ET[:, :], E[:, :], [ET, E], [pb])
            k.evac(PN[0][:, 0:128], pb[:, 0:128], [pb], [PN[0]])
            pb = k.ps()
            k.mm(pb[:, 0:128], E[:, :], ET[:, :], [ET, E], [pb])
            k.evac(PT[0][:, :], pb[:, 0:128], [pb], [PT[0]])
            cur = 0
            for lev in range(1, 6):
                PNc, PTc, PNn, PTn = PN[cur], PT[cur], PN[1 - cur], PT[1 - cur]
                pb = k.ps()
                if lev < 5:
                    k.mm(pb[:, 0:256], PTc[:, :], PNc[:, 0:256], [PTc, PNc], [pb])
                    k.evac(PNn[:, 0:128], pb[:, 0:128], [pb], [PNn])
                    k.tt(PNn[:, 128:256], pb[:, 128:256], PNc[:, 128:256], ALU.add, [pb, PNc], [PNn])
                    pb2 = k.ps()
                    k.mm(pb2[:, 0:128], PNc[:, 0:128], PTc[:, :], [PNc, PTc], [pb2])
                    k.evac(PTn[:, :], pb2[:, 0:128], [pb2], [PTn])
                else:
                    k.mm(pb[:, 0:128], PTc[:, :], PNc[:, 128:256], [PTc, PNc], [pb])
                    k.tt(PNn[:, 128:256], pb[:, 0:128], PNc[:, 128:256], ALU.add, [pb, PNc], [PNn])
                cur = 1 - cur
            return PN[cur]

        def transp(dst_ap, dst_tl, src_ap, src_tl, rows, scale=None):
            pb = k.ps()
            k.tr(pb[0:rows, 0:128], src_ap, ident, [src_tl, cst], [pb])
            if scale is None:
                k.evac(dst_ap, pb[0:rows, 0:128], [pb], [dst_tl])
            else:
                k.ts(dst_ap, pb[0:rows, 0:128], scale, None, ALU.mult, None, [pb], [dst_tl])

        xcur = xT_in
        for l in range(L):
            pv = k.sb("pv", [128, 256])
            k.dma(pv[:, :], pvec[l, :, :], [pvec], [pv])
            k.ts(pv[:, 0:24], pv[:, 0:24], 32.0, None, ALU.mult, None, [pv], [pv])

            with k.phase():
                wst = [k.sb("wst", [128, 2048]) for i in range(2)]
                WIN = k.sb("WIN", [128, 8 * (1024 + NTM)], BF16)
                wfm = lambda r, c0, cw: WIN[:, r * 1024 + c0: r * 1024 + c0 + cw]
                wtm = lambda r, c0, cw: WIN[:, 8192 + r * NTM + c0: 8192 + r * NTM + c0 + cw]
                load_w(wst, WIN, wfm, lambda r, c0, cw: w_in_fm[l, r * 128:(r + 1) * 128, c0:c0 + cw], 8, 1024)
                load_w(wst, WIN, wtm, lambda r, c0, cw: w_in_tm[l, r * 128:(r + 1) * 128, c0:c0 + cw], 8, NTM)
                if KSTOP == 10:
                    P.flush()
                    return nc, k
                xt = [k.sb("xt", [128, 8, TW]) for i in range(2)]
                sq = k.sb("sq", [128, 8, TW], BF16)
                hT = k.sb("hT", [128, 8, TW], BF16)
                rstd = k.sb("rstd", [128, TW])
                fm = k.sb("fm", [128, 8, TW])
                rot = k.sb("rot", [128, 4, TW], BF16)
                rt1 = k.sb("rt1", [128, TW])
                rcs = k.sb("rcs", [128, 2, TW])
                tmst = [k.sb("tmst", [128, NTM]) for i in range(2)]
                for w in range(NW):
                    t0 = w * TW
                    x_ = xt[w % 2]
                    k.dma(x_[:, :, :], xcur[:, t0:t0 + TW].rearrange("(c p) t -> p c t", p=128), [xcur], [x_])
                    k.dma(rcs[:, :, :], ropeCS[:, :, t0:t0 + TW].rearrange("a p t -> p a t"), [ropeCS], [rcs])
                    if KSTOP == 110:
                        P.flush()
                        return nc, k
                    if KSTOP == 111:
                        for c in range(8):
                            k.tt(sq[:, c, 0:TW], x_[:, c, 0:TW], x_[:, c, 0:TW], ALU.mult, [x_], [sq], eng="pool" if c % 2 else "dve")
                        P.flush()
                        return nc, k
                    if KSTOP == 112:
                        for c in range(8):
                            k.tt(sq[:, c, 0:TW], x_[:, c, 0:TW], x_[:, c, 0:TW], ALU.mult, [x_], [sq], eng="pool" if c % 2 else "dve")
                        pb = k.ps()
                        for c in range(8):
                            k.mm(pb[:, 0:TW], onesb, sq[:, c, 0:TW], [cbf, sq], [pb], start=(c == 0), stop=(c == 7))
                        k.cp(rstd[:, 0:TW], pb[:, 0:TW], [pb], [rstd])
                        P.flush()
                        return nc, k
                    rms_to_h(x_, hT, sq, rstd, pv, 0, TW)
                    if KSTOP == 11:
                        P.flush()
                        return nc, k
                    for j in range(8):
                        pb = k.ps()
                        for c in range(8):
                            k.mm(pb[:, 0:TW], WIN[:, c * 1024 + j * 128: c * 1024 + (j + 1) * 128], hT[:, c, :], [WIN, hT], [pb],
                                 start=(c == 0), stop=(c == 7))
                        k.evac(fm[:, j, :], pb[:, 0:TW], [pb], [fm])
                    if KSTOP == 12:
                        P.flush()
                        return nc, k
                    for j in range(4):
                        k.cp(rot[:, j, :], fm[:, j, :], [fm], [rot], eng="pool")
                        for base in (0, 64):
                            sl = slice(base, base + 16)
                            k.tt(rt1[sl, :], fm[sl, j, :], rcs[sl, 0, :], ALU.mult, [fm, rcs], [rt1])
                            k.tt(fm[sl, 4 + j, :], fm[sl, 4 + j, :], rcs[sl, 1, :], ALU.mult, [fm, rcs], [fm])
                            k.tt(rot[sl, j, :], rt1[sl, :], fm[sl, 4 + j, :], ALU.add, [rt1, fm], [rot])
                        off = 128 if j == 3 else 0
                        k.dma(qkr[j * 128:(j + 1) * 128, off + t0: off + t0 + TW], rot[:, j, :], [rot], [qkr])
                    if KSTOP == 13:
                        P.flush()
                        return nc, k
                    for s in range(SUB):
                        st_ = tmst[s % 2]
                        c0 = 0
                        while c0 < NTM:
                            cw = min(512, NTM - c0)
                            pb = k.ps()
                            for c in range(8):
                                k.mm(pb[:, 0:cw], hT[:, c, s * 128:(s + 1) * 128],
                                     WIN[:, 8192 + c * NTM + c0: 8192 + c * NTM + c0 + cw], [hT, WIN], [pb],
                                     start=(c == 0), stop=(c == 7))
                            k.evac(st_[:, c0:c0 + cw], pb[:, 0:cw], [pb], [st_])
                            c0 += cw
                        k.dma(ptok[2 + t0 + s * 128: 2 + t0 + (s + 1) * 128, :], st_[:, :], [st_], [ptok])

            if KSTOP == 1:
                P.flush()
                return nc, k
            if "att" in mixers:
                with k.phase():
                    qb = [k.sb("qb", [128, 3, 128], BF16) for i in range(2)]
                    kb = [k.sb("kb", [128, 384], BF16) for i in range(2)]
                    vb = [k.sb("vb", [128, 3, 128]) for i in range(2)]
                    vbb = [k.sb("vbb", [128, 3, 128], BF16) for i in range(2)]
                    sm = k.sb("sm", [128, 384])
                    pp = k.sb("pp", [128, 384], BF16)
                    pT = k.sb("pT", [128, 384], BF16)
                    st1 = k.sb("st1", [128, 8])
                    Otok = k.sb("Otok", [128, 384])
                    OT = k.sb("OT", [128, 3, 128], BF16)
                    prA = k.sb("prA", [128, 16])
                    k.dma(prA[:, 0:6], prow[l, :, PR_SINK:PR_SINK + 6], [prow], [prA])
                    k.ts(prA[:, 8:14], prA[:, 0:6], -1.0, None, ALU.mult, None, [prA], [prA])
                    for i in range(NT):
                        q_, k_, v_, vb_ = qb[i % 2], kb[i % 2], vb[i % 2], vbb[i % 2]
                        k.dma(q_[:, :, :], qkr[0:384, i * 128:(i + 1) * 128].rearrange("(c p) t -> p c t", p=128), [qkr], [q_])
                        k.dma(k_[:, :], qkr[384:512, i * 128: i * 128 + 384], [qkr], [k_])
                        for b3 in range(3):
                            blk = i - 1 + b3
                            if 0 <= blk < NT:
                                k.dma(v_[:, b3, :], ptok[2 + blk * 128: 2 + (blk + 1) * 128, C_AV:C_AV + 128], [ptok], [v_])
                            else:
                                k.memset(v_[:, b3, :], 0.0, [v_], eng="pool")
                        k.cp(vb_[:, :, :], v_[:, :, :], [v_], [vb_], eng="pool")
                        mi = 0 if i == 0 else (2 if i == NT - 1 else 1)
                        po = k.ps()
                        for h in range(NQ):
                            g = h // 3
                            cj = h % 3
                            ps_ = k.ps()
                            k.mm(ps_[:, 0:384], q_[g * 64:(g + 1) * 64, cj, :], k_[g * 64:(g + 1) * 64, :], [q_, k_], [ps_])
                            k.tt(sm[:, :], ps_[:, 0:384], amask[:, mi, :], ALU.add, [ps_, amask], [sm])
                            k.red(st1[:, 0:1], sm[:, :], ALU.max, [sm], [st1])
                            k.ts(st1[:, 1:2], st1[:, 0:1], -0.125, prA[:, 8 + h:9 + h], ALU.mult, ALU.min, [st1, prA], [st1])
                            k.act(pp[:, :], sm[:, :], AF.Exp, [sm, st1], [pp], bias=st1[:, 1:2], scale=0.125)
                            k.red(st1[:, 2:3], pp[:, :], ALU.add, [pp], [st1])
                            k.act(st1[:, 3:4], prA[:, h:h + 1], AF.Exp, [prA, st1], [st1], bias=st1[:, 1:2], scale=1.0)
                            k.tt(st1[:, 4:5], st1[:, 2:3], st1[:, 3:4], ALU.add, [st1], [st1])
                            k.recip(st1[:, 5:6], st1[:, 4:5], [st1], [st1])
                            pbb = k.bankb
                            for b3 in range(3):
                                k.tr(pbb[:, b3 * 128:(b3 + 1) * 128], pp[:, b3 * 128:(b3 + 1) * 128], identb, [pp, cbf], [pbb])
                            k.evac(pT[:, :], pbb[:, 0:384], [pbb], [pT])
                            for b3 in range(3):
                                k.mm(po[:, h * 64:(h + 1) * 64], pT[:, b3 * 128:(b3 + 1) * 128], vb_[:, b3, g * 64:(g + 1) * 64],
                                     [pT, vb_], [po], start=(b3 == 0), stop=(b3 == 2))
                            k.ts(Otok[:, h * 64:(h + 1) * 64], po[:, h * 64:(h + 1) * 64], st1[:, 5:6], None, ALU.mult, None,
                                 [po, st1], [Otok])
                        pt_ = k.ps()
                        for c in range(3):
                            k.tr(pt_[:, c * 128:(c + 1) * 128], Otok[:, c * 128:(c + 1) * 128], ident, [Otok, cst], [pt_])
                        k.evac(OT[:, :, :], pt_[:, 0:384].rearrange("p (c t) -> p c t", c=3), [pt_], [OT])
                        k.dma(mixT[0:384, i * 128:(i + 1) * 128].rearrange("(c p) t -> p c t", p=128), OT[:, :, :], [OT], [mixT])
            zero_chunks = ([] if "att" in mixers else [0, 1, 2]) + ([] if "dn" in mixers else [3, 4, 5]) + \
                          ([] if "rw" in mixers else [6, 7])
            for c in zero_chunks:
                for t0 in range(0, T, 2048):
                    tw = min(2048, T - t0)
                    k.dma(mixT[c * 128:(c + 1) * 128, t0:t0 + tw], ztb[:, 0:tw], [ztf], [mixT])

            if "dn" in mixers:
                with k.phase():
                    cw_ = k.sb("cw", [128, 5, 1152])
                    k.dma(cw_[:, :, :], prow[l, :, PR_CONV:PR_CONV + 5760].rearrange("p (a c) -> p a c", a=5), [prow], [cw_])
                    prD = k.sb("prD", [128, 416])
                    k.dma(prD[:, 0:408], prow[l, :, PR_ALOG:PR_ALOG + 408], [prow], [prD])
                    Aexp = k.sb("Aexp", [128, 12])
                    k.act(Aexp[:, :], prD[:, 0:12], AF.Exp, [prD], [Aexp])
                    xa = [k.sb("xa", [128, 1152]) for i in range(2)]
                    acc = k.sb("acc", [128, 1152])
                    tmp = k.sb("tmpc", [128, 1152])
                    qkv = k.sb("qkv", [128, 1152])
                    sqv = k.sb("sqv", [128, 768])
                    sm12 = k.sb("sm12", [128, 96])
                    ab = k.sb("ab", [128, 24])
                    gcs = k.sb("gcs", [128, 64])
                    kT = k.sb("kT", [64, 128], BF16)
                    qT = k.sb("qT", [64, 128], BF16)
                    qgT = k.sb("qgT", [64, 128], BF16)
                    qg = k.sb("qg", [128, 64])
                    Gl = k.sb("Gl", [128, 128])
                    DE = k.sb("DE", [128, 128])
                    E1 = k.sb("E1", [128, 128])
                    E = k.sb("E", [128, 128], BF16)
                    ET = k.sb("ET", [128, 128], BF16)
                    LY = k.sb("LY", [128, 128], BF16)
                    PN = [k.sb("PN", [128, 256], BF16) for i in range(2)]
                    PT = [k.sb("PT", [128, 128], BF16) for i in range(2)]
                    VK = k.sb("VK", [128, 128], BF16)
                    UVb = k.sb("UVb", [128, 64])
                    LA = k.sb("LA", [64, 128], BF16)
                    LB = k.sb("LB", [128, 64], BF16)
                    vnb = k.sb("vnb", [128, 64], BF16)
                    S = k.sb("S", [64, 6, 64])
                    Sb = k.sb("Sb", [64, 6, 64], BF16)
                    otile = k.sb("otile", [128, 384])
                    oprev = k.sb("oprev", [128, 384])
                    zt_ = k.sb("zt_", [128, 384])
                    osq = k.sb("osq", [128, 384])
                    oT = k.sb("oT", [128, 3, 128], BF16)
                    for d in (0, 1):
                        Ucol = CA if d == 0 else CC
                        Gcol = CD if d == 0 else CB
                        Mcol = CM0 if d == 0 else CM1
                        k.memset(S[:, :, :], 0.0, [S])
                        k.memset(Sb[:, :, :], 0.0, [Sb], eng="pool")
                        order = range(NT) if d == 0 else range(NT - 1, -1, -1)
                        for i in order:
                            r0 = 2 + i * 128
                            for a in range(5):
                                x_ = xa[a % 2]
                                k.dma(x_[:, :], ptok[r0 + a - 2: r0 + a - 2 + 128, C_DQKV:C_DQKV + 1152], [ptok], [x_])
                                if a == 0:
                                    k.tt(acc[:, :], x_[:, :], cw_[:, 0, :], ALU.mult, [x_, cw_], [acc])
                                else:
                                    k.tt(tmp[:, :], x_[:, :], cw_[:, a, :], ALU.mult, [x_, cw_], [tmp], eng="pool")
                                    k.tt(acc[:, :], acc[:, :], tmp[:, :], ALU.add, [acc, tmp], [acc])
                            k.act(qkv[:, :], acc[:, :], AF.Silu, [acc], [qkv])
                            k.tt(sqv[:, :], qkv[:, 0:768], qkv[:, 0:768], ALU.mult, [qkv], [sqv], eng="pool")
                            for hh in range(12):
                                k.red(sm12[:, hh:hh + 1], sqv[:, hh * 64:(hh + 1) * 64], ALU.add, [sqv], [sm12])
                            k.act(sm12[:, 12:24], sm12[:, 0:12], AF.Ln, [sm12, cst], [sm12], bias=cst[:, CMISC + 4:CMISC + 5], scale=1.0)
                            k.act(sm12[:, 12:24], sm12[:, 12:24], AF.Exp, [sm12], [sm12], scale=-0.5)
                            k.ts(sm12[:, 12:18], sm12[:, 12:18], 0.125, None, ALU.mult, None, [sm12], [sm12])
                            for hh in range(12):
                                k.ts(qkv[:, hh * 64:(hh + 1) * 64], qkv[:, hh * 64:(hh + 1) * 64], sm12[:, 12 + hh:13 + hh], None,
                                     ALU.mult, None, [qkv, sm12], [qkv], eng="pool" if hh % 2 else "dve")
                            k.dma(ab[:, :], ptok[r0:r0 + 128, C_DAL:C_DAL + 24], [ptok], [ab])
                            k.tt(sm12[:, 24:36], ab[:, 0:12], prD[:, 12:24], ALU.add, [ab, prD], [sm12])
                            k.act(sm12[:, 24:36], sm12[:, 24:36], AF.Exp, [sm12], [sm12])
                            k.act(sm12[:, 24:36], sm12[:, 24:36], AF.Ln, [sm12], [sm12], bias=1.0, scale=1.0)
                            k.stt(sm12[:, 24:36], sm12[:, 24:36], -1.0, Aexp[:, :], ALU.mult, ALU.mult, [sm12, Aexp], [sm12])
                            k.act(sm12[:, 36:48], ab[:, 12:24], AF.Sigmoid, [ab], [sm12])
                            gsl = sm12[:, 24 + d * 6: 30 + d * 6]
                            pb = k.ps()
                            k.mm(pb[:, 0:6], cst[:, Ucol:Ucol + 128], gsl, [cst, sm12], [pb])
                            k.mm(pb[:, 8:14], cst[:, CSC:CSC + 128], gsl, [cst, sm12], [pb])
                            k.mm(pb[:, 16:22], cst[:, CNSC:CNSC + 128], gsl, [cst, sm12], [pb])
                            k.cp(gcs[:, 0:24], pb[:, 0:24], [pb], [gcs])
                            k.act(gcs[:, 24:30], gcs[:, 0:6], AF.Exp, [gcs], [gcs])
                            k.tt(gcs[:, 32:38], gcs[:, 8:14], gcs[:, 0:6], ALU.subtract, [gcs], [gcs])
                            k.act(gcs[:, 32:38], gcs[:, 32:38], AF.Exp, [gcs], [gcs])
                            k.act(gcs[:, 40:46], gcs[:, 8:14], AF.Exp, [gcs], [gcs])
                            k.act(gcs[:, 48:54], gcs[:, 16:22], AF.Exp, [gcs], [gcs])
                            for h in range(6):
                                qh = qkv[:, h * 64:(h + 1) * 64]
                                kh = qkv[:, 384 + h * 64: 384 + (h + 1) * 64]
                                vh = qkv[:, 768 + h * 64: 768 + (h + 1) * 64]
                                bcol = sm12[:, 36 + d * 6 + h: 37 + d * 6 + h]
                                wcol = gcs[:, 24 + h:25 + h]
                                transp(kT[:, :], kT, kh, qkv, 64)
                                transp(qT[:, :], qT, qh, qkv, 64)
                                k.ts(qg[:, :], qh, wcol, None, ALU.mult, None, [qkv, gcs], [qg])
                                transp(qgT[:, :], qgT, qg[:, :], qg, 64)
                                k.ts(Gl[:, :], cst[:, Gcol:Gcol + 128], sm12[:, 24 + d * 6 + h:25 + d * 6 + h], None, ALU.mult, None,
                                     [cst, sm12], [Gl], eng="pool")
                                pbd = k.ps()
                                k.mm(pbd[:, 0:128], Gl[:, :], cst[:, Ucol:Ucol + 128], [Gl, cst], [pbd])
                                k.act(DE[:, :], pbd[:, 0:128], AF.Exp, [pbd], [DE])
                                pbk = k.ps()
                                k.mm(pbk[:, 0:128], kT[:, :], kT[:, :], [kT], [pbk])
                                k.mm(pbk[:, 128:256], kT[:, :], qT[:, :], [kT, qT], [pbk])
                                k.tt(E1[:, :], pbk[:, 0:128], DE[:, :], ALU.mult, [pbk, DE], [E1])
                                k.stt(E[:, :], E1[:, :], bcol, cst[:, Mcol:Mcol + 128], ALU.mult, ALU.mult, [E1, sm12, cst], [E])
                                k.tt(E1[:, :], pbk[:, 128:256], DE[:, :], ALU.mult, [pbk, DE], [E1])
                                k.tt(LY[:, :], E1[:, :], cst[:, Mcol + 128:Mcol + 256], ALU.mult, [E1, cst], [LY], eng="pool")
                                pbb = k.bankb
                                k.tr(pbb[:, 0:128], E[:, :], identb, [E, cbf], [pbb])
                                k.evac(ET[:, :], pbb[:, 0:128], [pbb], [ET])
                                Nt = invert(E, ET, PN, PT)
                                N_ = Nt[:, 128:256]
                                k.cp(VK[:, 0:64], vh, [qkv], [VK], eng="pool")
                                k.ts(VK[:, 64:128], kh, wcol, None, ALU.mult, None, [qkv, gcs], [VK])
                                k.ts(LB[:, :], kh, gcs[:, 32 + h:33 + h], None, ALU.mult, None, [qkv, gcs], [LB], eng="pool")
                                pbv = k.ps()
                                k.mm(pbv[:, 0:64], N_, VK[:, 0:64], [Nt, VK], [pbv])
                                k.ts(UVb[:, :], pbv[:, 0:64], bcol, None, ALU.mult, None, [pbv, sm12], [UVb])
                                pbl = k.ps()
                                k.mm(pbl[0:64, 0:128], VK[:, 64:128], N_, [Nt, VK], [pbl])
                                k.ts(LA[:, :], pbl[0:64, 0:128], -1.0, None, ALU.mult, None, [pbl], [LA])
                                for ci in ((0, 1) if d == 0 else (1, 0)):
                                    rs = slice(ci * 64, ci * 64 + 64)
                                    glc = gcs[0:64, 40 + h:41 + h] if ci == 0 else gcs[0:64, 48 + h:49 + h]
                                    pa = k.ps()
                                    k.mm(pa[:, 0:64], LA[:, :], Sb[:, h, :], [LA, Sb], [pa])
                                    k.stt(vnb[rs, :], pa[rs, 0:64], sm12[rs, 36 + d * 6 + h: 37 + d * 6 + h], UVb[rs, :], ALU.mult, ALU.add,
                                          [pa, sm12, UVb], [vnb])
                                    po_ = k.ps()
                                    k.mm(po_[:, 0:64], qgT[:, :], Sb[:, h, :], [qgT, Sb], [po_], start=True, stop=False)
                                    k.mm(po_[:, 0:64], LY[rs, :], vnb[rs, :], [LY, vnb], [po_], start=False, stop=True)
                                    k.cp(otile[rs, h * 64:(h + 1) * 64], po_[rs, 0:64], [po_], [otile], eng="act")
                                    pS = k.ps()
                                    k.mm(pS[0:64, 0:64], LB[rs, :], vnb[rs, :], [LB, vnb], [pS])
                                    k.stt(S[:, h, :], S[:, h, :], glc, pS[0:64, 0:64], ALU.mult, ALU.add, [S, gcs, pS], [S])
                                    k.cp(Sb[:, h, :], S[:, h, :], [S], [Sb], eng="pool")
                            if d == 0:
                                k.dma(ofw[i * 128:(i + 1) * 128, 0:384], otile[:, :], [otile], [ofw])
                            else:
                                k.dma(oprev[:, :], ofw[i * 128:(i + 1) * 128, 0:384], [ofw], [oprev])
                                k.dma(zt_[:, :], ptok[r0:r0 + 128, C_DZ:C_DZ + 384], [ptok], [zt_])
                                k.tt(otile[:, :], otile[:, :], oprev[:, :], ALU.add, [otile, oprev], [otile])
                                k.tt(osq[:, :], otile[:, :], otile[:, :], ALU.mult, [otile], [osq], eng="pool")
                                for h in range(6):
                                    k.red(sm12[:, 48 + h:49 + h], osq[:, h * 64:(h + 1) * 64], ALU.add, [osq], [sm12])
                                k.act(sm12[:, 56:62], sm12[:, 48:54], AF.Ln, [sm12, cst], [sm12], bias=cst[:, CMISC + 5:CMISC + 6], scale=1.0)
                                k.act(sm12[:, 56:62], sm12[:, 56:62], AF.Exp, [sm12], [sm12], scale=-0.5)
                                k.ts(sm12[:, 56:62], sm12[:, 56:62], 8.0, None, ALU.mult, None, [sm12], [sm12])
                                for h in range(6):
                                    k.ts(otile[:, h * 64:(h + 1) * 64], otile[:, h * 64:(h + 1) * 64], sm12[:, 56 + h:57 + h], None,
                                         ALU.mult, None, [otile, sm12], [otile])
                                k.tt(otile[:, :], otile[:, :], prD[:, 24:408], ALU.mult, [otile, prD], [otile])
                                k.act(osq[:, :], zt_[:, :], AF.Silu, [zt_], [osq])
                                k.tt(otile[:, :], otile[:, :], osq[:, :], ALU.mult, [otile, osq], [otile])
                                pt_ = k.ps()
                                for c in range(3):
                                    k.tr(pt_[:, c * 128:(c + 1) * 128], otile[:, c * 128:(c + 1) * 128], ident, [otile, cst], [pt_])
                                k.evac(oT[:, :, :], pt_[:, 0:384].rearrange("p (c t) -> p c t", c=3), [pt_], [oT])
                                k.dma(mixT[384:768, i * 128:(i + 1) * 128].rearrange("(c p) t -> p c t", p=128), oT[:, :, :], [oT], [mixT])

            if "rw" in mixers:
                with k.phase():
                    prR = k.sb("prR", [128, NPR - PR_MU0])
                    k.dma(prR[:, :], prow[l, :, PR_MU0:NPR], [prow], [prR])
                    o_ = lambda off: off - PR_MU0
                    omka = k.sb("omka", [128, 256])
                    k.ts(omka[:, :], prR[:, o_(PR_KA):o_(PR_KA) + 256], -1.0, 1.0, ALU.mult, ALU.add, [prR], [omka])
                    wtmp = k.sb("wtmp", [128, 256])
                    wupb_ = k.sb("wupb_", [64, 2, 256], BF16)
                    aupP = k.sb("aupP", [128, 2, 256], BF16)
                    gupb_ = k.sb("gupb_", [128, 256], BF16)
                    for dd in range(2):
                        k.dma(wtmp[0:64, :], rw_wup[l, dd, :, :], [rw_wup], [wtmp])
                        k.cp(wupb_[:, dd, :], wtmp[0:64, :], [wtmp], [wupb_])
                        k.dma(wtmp[64:128, :], rw_aup[l, dd, :, :], [rw_aup], [wtmp])
                        k.cp(aupP[64:128, dd, :], wtmp[64:128, :], [wtmp], [aupP])
                    k.dma(wtmp[:, :], rw_gup[l, :, :], [rw_gup], [wtmp])
                    k.cp(gupb_[:, :], wtmp[:, :], [wtmp], [gupb_])
                    pc = k.sb("pc", [128, 1024])
                    pp_ = k.sb("pprev", [128, 1024])
                    pn_ = k.sb("pnext", [128, 1024])
                    kkn = k.sb("kkn", [128, 256])
                    gate = k.sb("gate", [128, 256])
                    sm8 = k.sb("sm8", [128, 32])
                    tT = k.sb("tT", [128, 128], BF16)
                    lw = k.sb("lw", [128, 256])
                    av = k.sb("av", [128, 256])
                    kd = k.sb("kd", [128, 256])
                    bb = k.sb("bb", [128, 256])
                    Linc = k.sb("Linc", [128, 256])
                    Lexc = k.sb("Lexc", [128, 256])
                    Ltot = k.sb("Ltot", [128, 256])
                    Loth = k.sb("Loth", [128, 256])
                    at_ = k.sb("at_", [128, 256])
                    bt_ = k.sb("bt_", [128, 256])
                    kt_ = k.sb("kt_", [128, 256])
                    rt_ = k.sb("rt_", [128, 256])
                    Bh = k.sb("Bh", [128, 256], BF16)
                    Kh = k.sb("Kh", [128, 256], BF16)
                    atb = k.sb("atb", [128, 256], BF16)
                    vbf = k.sb("vbf", [128, 256], BF16)
                    wct = k.sb("wct", [128, 256])
                    wco = k.sb("wco", [128, 256])
                    aT = k.sb("aT", [64, 256], BF16)
                    bT = k.sb("bT", [64, 128], BF16)
                    kT2 = k.sb("kT2", [64, 128], BF16)
                    wcT = k.sb("wcT", [64, 256])
                    E = k.sb("Er", [128, 128], BF16)
                    ET = k.sb("ETr", [128, 128], BF16)
                    NRB = k.sb("NRB", [128, 128], BF16)
                    AK = k.sb("AK", [128, 128], BF16)
                    RK = k.sb("RK", [128, 128], BF16)
                    PN = [k.sb("PNr", [128, 256], BF16) for i in range(2)]
                    PT = [k.sb("PTr", [128, 128], BF16) for i in range(2)]
                    Xb = k.sb("Xb", [128, 64], BF16)
                    UV = k.sb("UVr", [128, 64])
                    LA = k.sb("LAr", [64, 128], BF16)
                    Ub = k.sb("Ub", [128, 64], BF16)
                    H = k.sb("H", [64, 4, 64])
                    Hb = k.sb("Hb", [64, 4, 64], BF16)
                    ytile = k.sb("ytile", [128, 256])
                    yprev = k.sb("yprev", [128, 256])
                    ysq = k.sb("ysq", [128, 256])
                    yT = k.sb("yT", [128, 2, 128], BF16)
                    for d in (0, 1):
                        Ucol = CA if d == 0 else CC
                        Xcol = CB if d == 0 else CD
                        Mcol = CM0 if d == 0 else CM1
                        k.memset(H[:, :, :], 0.0, [H])
                        k.memset(Hb[:, :, :], 0.0, [Hb], eng="pool")
                        order = range(NT) if d == 0 else range(NT - 1, -1, -1)
                        for i in order:
                            r0 = 2 + i * 128
                            k.dma(pc[:, :], ptok[r0:r0 + 128, C_RW:C_RW + 1024], [ptok], [pc])
                            k.dma(pp_[:, :], ptok[r0 - 1:r0 + 127, C_RW:C_RW + 1024], [ptok], [pp_])
                            k.dma(pn_[:, :], ptok[r0 + 1:r0 + 129, C_RW:C_RW + 1024], [ptok], [pn_])
                            k.tt(pp_[:, :], pp_[:, :], pc[:, :], ALU.subtract, [pp_, pc], [pp_])
                            k.tt(pn_[:, :], pn_[:, :], pc[:, :], ALU.subtract, [pn_, pc], [pn_], eng="pool")
                            k.tt(pp_[:, :], pp_[:, :], prR[:, o_(PR_MU0):o_(PR_MU0) + 1024], ALU.mult, [pp_, prR], [pp_])
                            k.tt(pn_[:, :], pn_[:, :], prR[:, o_(PR_MU1):o_(PR_MU1) + 1024], ALU.mult, [pn_, prR], [pn_], eng="pool")
                            k.tt(pc[:, :], pc[:, :], pp_[:, :], ALU.add, [pc, pp_], [pc])
                            k.tt(pc[:, :], pc[:, :], pn_[:, :], ALU.add, [pc, pn_], [pc])
                            r_, k_, v_ = pc[:, 0:256], pc[:, 256:512], pc[:, 512:768]
                            k.act(pc[:, 768:832], pc[:, 768:832], AF.Tanh, [pc], [pc])
                            k.act(pc[:, 896:1024], pc[:, 896:1024], AF.Sigmoid, [pc], [pc])
                            transp(tT[:, :], tT, pc[:, 896:1024], pc, 128)
                            pb = k.ps()
                            k.mm(pb[:, 0:256], tT[:, :], gupb_[:, :], [tT, gupb_], [pb])
                            k.cp(gate[:, :], pb[:, 0:256], [pb], [gate], eng="act")
                            transp(tT[:, :], tT, pc[:, 768:896], pc, 128)
                            pb = k.ps()
                            k.mm(pb[:, 0:256], tT[0:64, :], wupb_[:, d, :], [tT, wupb_], [pb])
                            k.tt(lw[:, :], pb[:, 0:256], prR[:, o_(PR_W0) + d * 256:o_(PR_W0) + (d + 1) * 256], ALU.add, [pb, prR], [lw])
                            pb2 = k.ps()
                            k.mm(pb2[:, 0:256], tT[64:128, :], aupP[64:128, d, :], [tT, aupP], [pb2])
                            k.tt(av[:, :], pb2[:, 0:256], prR[:, o_(PR_A0) + d * 256:o_(PR_A0) + (d + 1) * 256], ALU.add, [pb2, prR], [av])
                            k.act(av[:, :], av[:, :], AF.Sigmoid, [av], [av])
                            k.act(lw[:, :], lw[:, :], AF.Exp, [lw], [lw], scale=-1.0)
                            k.act(lw[:, :], lw[:, :], AF.Ln, [lw], [lw], bias=1.0, scale=1.0)
                            k.act(lw[:, :], lw[:, :], AF.Exp, [lw, cst], [lw], bias=cst[:, CMISC + 6:CMISC + 7], scale=-1.0)
                            k.ts(lw[:, :], lw[:, :], -1.0, None, ALU.mult, None, [lw], [lw])
                            k.tt(kkn[:, :], k_, prR[:, o_(PR_KK):o_(PR_KK) + 256], ALU.mult, [pc, prR], [kkn])
                            k.tt(ysq[:, :], kkn[:, :], kkn[:, :], ALU.mult, [kkn], [ysq], eng="pool")
                            for h in range(4):
                                k.red(sm8[:, h:h + 1], ysq[:, h * 64:(h + 1) * 64], ALU.add, [ysq], [sm8])
                            k.act(sm8[:, 4:8], sm8[:, 0:4], AF.Ln, [sm8, cst], [sm8], bias=cst[:, CMISC + 4:CMISC + 5], scale=1.0)
                            k.act(sm8[:, 4:8], sm8[:, 4:8], AF.Exp, [sm8], [sm8], scale=-0.5)
                            for h in range(4):
                                k.ts(kkn[:, h * 64:(h + 1) * 64], kkn[:, h * 64:(h + 1) * 64], sm8[:, 4 + h:5 + h], None, ALU.mult, None,
                                     [kkn, sm8], [kkn])
                            k.tt(kd[:, :], av[:, :], prR[:, o_(PR_KA):o_(PR_KA) + 256], ALU.mult, [av, prR], [kd])
                            k.tt(kd[:, :], kd[:, :], omka[:, :], ALU.add, [kd, omka], [kd])
                            k.tt(kd[:, :], kd[:, :], k_, ALU.mult, [kd, pc], [kd])
                            k.tt(bb[:, :], kkn[:, :], av[:, :], ALU.mult, [kkn, av], [bb], eng="pool")
                            for (dst, col) in ((Linc, Ucol), (Lexc, Xcol), (Ltot, CSC), (Loth, CNSC)):
                                pb = k.ps()
                                k.mm(pb[:, 0:256], cst[:, col:col + 128], lw[:, :], [cst, lw], [pb])
                                k.evac(dst[:, :], pb[:, 0:256], [pb], [dst])
                            k.act(wct[:, :], Ltot[:, :], AF.Exp, [Ltot], [wct])
                            k.act(wco[:, :], Loth[:, :], AF.Exp, [Loth], [wco])
                            k.tt(Ltot[:, :], Ltot[:, :], Linc[:, :], ALU.subtract, [Ltot, Linc], [Ltot])
                            k.act(Ltot[:, :], Ltot[:, :], AF.Exp, [Ltot], [Ltot])
                            k.act(Lexc[:, :], Lexc[:, :], AF.Exp, [Lexc], [Lexc])
                            k.act(Loth[:, :], Linc[:, :], AF.Exp, [Linc], [Loth], scale=-1.0)
                            k.act(Linc[:, :], Linc[:, :], AF.Exp, [Linc], [Linc])
                            k.tt(at_[:, :], kkn[:, :], Lexc[:, :], ALU.mult, [kkn, Lexc], [at_])
                            k.tt(bt_[:, :], bb[:, :], Loth[:, :], ALU.mult, [bb, Loth], [bt_], eng="pool")
                            k.tt(kt_[:, :], kd[:, :], Loth[:, :], ALU.mult, [kd, Loth], [kt_])
                            k.tt(rt_[:, :], r_, Linc[:, :], ALU.mult, [pc, Linc], [rt_], eng="pool")
                            k.stt(Bh[:, :], bb[:, :], -1.0, Ltot[:, :], ALU.mult, ALU.mult, [bb, Ltot], [Bh])
                            k.tt(Kh[:, :], kd[:, :], Ltot[:, :], ALU.mult, [kd, Ltot], [Kh], eng="pool")
                            k.cp(atb[:, :], at_[:, :], [at_], [atb], eng="pool")
                            k.cp(vbf[:, :], v_, [pc], [vbf])
                            for h in range(4):
                                hs = slice(h * 64, (h + 1) * 64)
                                transp(aT[:, 0:128], aT, at_[:, hs], at_, 64)
                                transp(aT[:, 128:256], aT, rt_[:, hs], rt_, 64)
                                transp(bT[:, :], bT, bt_[:, hs], bt_, 64)
                                transp(kT2[:, :], kT2, kt_[:, hs], kt_, 64)
                                transp(wcT[:, 0:128], wcT, wct[:, hs], wct, 64)
                                transp(wcT[:, 128:256], wcT, wco[:, hs], wco, 64)
                                pb = k.ps()
                                k.mm(pb[:, 0:256], bT[:, :], aT[:, :], [bT, aT], [pb])
                                k.tt(E[:, :], pb[:, 0:128], cst[:, Mcol:Mcol + 128], ALU.mult, [pb, cst], [E])
                                k.stt(NRB[:, :], pb[:, 128:256], -1.0, cst[:, Mcol + 128:Mcol + 256], ALU.mult, ALU.mult, [pb, cst], [NRB])
                                pb = k.ps()
                                k.mm(pb[:, 0:256], kT2[:, :], aT[:, :], [kT2, aT], [pb])
                                k.tt(AK[:, :], pb[:, 0:128], cst[:, Mcol:Mcol + 128], ALU.mult, [pb, cst], [AK])
                                k.tt(RK[:, :], pb[:, 128:256], cst[:, Mcol + 128:Mcol + 256], ALU.mult, [pb, cst], [RK])
                                pbb = k.bankb
                                k.tr(pbb[:, 0:128], E[:, :], identb, [E, cbf], [pbb])
                                k.evac(ET[:, :], pbb[:, 0:128], [pbb], [ET])
                                Nt = invert(E, ET, PN, PT)
                                N_ = Nt[:, 128:256]
                                pb = k.ps()
                                k.mm(pb[:, 0:64], AK[:, :], vbf[:, hs], [AK, vbf], [pb])
                                k.evac(Xb[:, :], pb[:, 0:64], [pb], [Xb])
                                pb = k.ps()
                                k.mm(pb[:, 0:64], N_, Xb[:, :], [Nt, Xb], [pb])
                                k.evac(UV[:, :], pb[:, 0:64], [pb], [UV])
                                pb = k.ps()
                                k.mm(pb[0:64, 0:128], atb[:, hs], N_, [atb, Nt], [pb])
                                k.evac(LA[:, :], pb[0:64, 0:128], [pb], [LA])
                                for ci in ((0, 1) if d == 0 else (1, 0)):
                                    rs = slice(ci * 64, ci * 64 + 64)
                                    wcc = wcT[:, 0:1] if ci == 0 else wcT[:, 128:129]
                                    pa = k.ps()
                                    k.mm(pa[:, 0:64], LA[:, :], Hb[:, h, :], [LA, Hb], [pa])
                                    k.tt(Ub[rs, :], pa[rs, 0:64], UV[rs, :], ALU.add, [pa, UV], [Ub])
                                    po_ = k.ps()
                                    k.mm(po_[:, 0:64], aT[:, 128:256], Hb[:, h, :], [aT, Hb], [po_], start=True, stop=False)
                                    k.mm(po_[:, 0:64], NRB[rs, :], Ub[rs, :], [NRB, Ub], [po_], start=False, stop=False)
                                    k.mm(po_[:, 0:64], RK[rs, :], vbf[rs, hs], [RK, vbf], [po_], start=False, stop=True)
                                    k.cp(ytile[rs, hs], po_[rs, 0:64], [po_], [ytile], eng="act")
                                    pS = k.ps()
                                    k.mm(pS[0:64, 0:64], Bh[rs, hs], Ub[rs, :], [Bh, Ub], [pS], start=True, stop=False)
                                    k.mm(pS[0:64, 0:64], Kh[rs, hs], vbf[rs, hs], [Kh, vbf], [pS], start=False, stop=True)
                                    k.stt(H[:, h, :], H[:, h, :], wcc, pS[0:64, 0:64], ALU.mult, ALU.add, [H, wcT, pS], [H])
                                    k.cp(Hb[:, h, :], H[:, h, :], [H], [Hb], eng="pool")
                            if d == 0:
                                k.dma(ofw[i * 128:(i + 1) * 128, 384:640], ytile[:, :], [ytile], [ofw])
                            else:
                                k.dma(yprev[:, :], ofw[i * 128:(i + 1) * 128, 384:640], [ofw], [yprev])
                                k.tt(ytile[:, :], ytile[:, :], yprev[:, :], ALU.add, [ytile, yprev], [ytile])
                                for h in range(4):
                                    k.red(sm8[:, 8 + h:9 + h], ytile[:, h * 64:(h + 1) * 64], ALU.add, [ytile], [sm8])
                                k.ts(sm8[:, 8:12], sm8[:, 8:12], -1.0 / 64, None, ALU.mult, None, [sm8], [sm8])
                                for h in range(4):
                                    k.ts(ytile[:, h * 64:(h + 1) * 64], ytile[:, h * 64:(h + 1) * 64], sm8[:, 8 + h:9 + h], None, ALU.add, None,
                                         [ytile, sm8], [ytile])
                                k.tt(ysq[:, :], ytile[:, :], ytile[:, :], ALU.mult, [ytile], [ysq], eng="pool")
                                for h in range(4):
                                    k.red(sm8[:, 12 + h:13 + h], ysq[:, h * 64:(h + 1) * 64], ALU.add, [ysq], [sm8])
                                k.act(sm8[:, 12:16], sm8[:, 12:16], AF.Ln, [sm8, cst], [sm8], bias=cst[:, CMISC + 7:CMISC + 8], scale=1.0)
                                k.act(sm8[:, 12:16], sm8[:, 12:16], AF.Exp, [sm8], [sm8], scale=-0.5)
                                k.ts(sm8[:, 12:16], sm8[:, 12:16], 8.0, None, ALU.mult, None, [sm8], [sm8])
                                for h in range(4):
                                    k.ts(ytile[:, h * 64:(h + 1) * 64], ytile[:, h * 64:(h + 1) * 64], sm8[:, 12 + h:13 + h], None, ALU.mult, None,
                                         [ytile, sm8], [ytile])
                                k.tt(ytile[:, :], ytile[:, :], prR[:, o_(PR_LNW):o_(PR_LNW) + 256], ALU.mult, [ytile, prR], [ytile])
                                k.tt(ytile[:, :], ytile[:, :], prR[:, o_(PR_LNB):o_(PR_LNB) + 256], ALU.add, [ytile, prR], [ytile])
                                k.tt(ysq[:, :], r_, k_, ALU.mult, [pc], [ysq], eng="pool")
                                k.tt(ysq[:, :], ysq[:, :], prR[:, o_(PR_RK):o_(PR_RK) + 256], ALU.mult, [ysq, prR], [ysq], eng="pool")
                                for h in range(4):
                                    k.red(sm8[:, 16 + h:17 + h], ysq[:, h * 64:(h + 1) * 64], ALU.add, [ysq], [sm8])
                                for h in range(4):
                                    k.stt(ytile[:, h * 64:(h + 1) * 64], pc[:, 512 + h * 64:512 + (h + 1) * 64], sm8[:, 16 + h:17 + h],
                                          ytile[:, h * 64:(h + 1) * 64], ALU.mult, ALU.add, [pc, sm8, ytile], [ytile])
                                k.tt(ytile[:, :], ytile[:, :], gate[:, :], ALU.mult, [ytile, gate], [ytile])
                                pt_ = k.ps()
                                for c in range(2):
                                    k.tr(pt_[:, c * 128:(c + 1) * 128], ytile[:, c * 128:(c + 1) * 128], ident, [ytile, cst], [pt_])
                                k.evac(yT[:, :, :], pt_[:, 0:256].rearrange("p (c t) -> p c t", c=2), [pt_], [yT])
                                k.dma(mixT[768:1024, i * 128:(i + 1) * 128].rearrange("(c p) t -> p c t", p=128), yT[:, :, :], [yT], [mixT])

            with k.phase():
                wst = [k.sb("wst", [128, 2048]) for i in range(2)]
                WO = k.sb("WO", [128, 8 * 1024], BF16)
                wo = lambda r, c0, cw: WO[:, r * 1024 + c0: r * 1024 + c0 + cw]
                load_w(wst, WO, wo, lambda r, c0, cw: w_out[l, r * 128:(r + 1) * 128, c0:c0 + cw], 8, 1024)
                wcast = [k.sb("wcast", [128, 2048], BF16) for i in range(2)]
                n_ = 0
                for r in range(8):
                    for c0 in range(0, 2 * DFF, 2048):
                        cw = min(2048, 2 * DFF - c0)
                        s_ = wst[n_ % 2]
                        c_ = wcast[n_ % 2]
                        n_ += 1
                        k.dma(s_[:, 0:cw], w_up[l, r * 128:(r + 1) * 128, c0:c0 + cw], [], [s_])
                        k.cp(c_[:, 0:cw], s_[:, 0:cw], [s_], [c_], eng="pool")
                        k.dma(wupb[r * 128:(r + 1) * 128, c0:c0 + cw], c_[:, 0:cw], [c_], [wupb])
                xt = [k.sb("xt", [128, 8, TW]) for i in range(2)]
                mx = [k.sb("mx", [128, 8, TW], BF16) for i in range(2)]
                xn = [k.sb("xn", [128, 8, TW]) for i in range(2)]
                sq = k.sb("sq", [128, 8, TW], BF16)
                hT = k.sb("hT", [128, 8, TW], BF16)
                rstd = k.sb("rstd", [128, TW])
                for w in range(NW):
                    t0 = w * TW
                    x_, m_, xn_ = xt[w % 2], mx[w % 2], xn[w % 2]
                    k.dma(x_[:, :, :], xcur[:, t0:t0 + TW].rearrange("(c p) t -> p c t", p=128), [xcur], [x_])
                    k.dma(m_[:, :, :], mixT[:, t0:t0 + TW].rearrange("(c p) t -> p c t", p=128), [mixT], [m_])
                    for j in range(8):
                        pb = k.ps()
                        for c in range(8):
                            k.mm(pb[:, 0:TW], WO[:, c * 1024 + j * 128: c * 1024 + (j + 1) * 128], m_[:, c, :], [WO, m_], [pb],
                                 start=(c == 0), stop=(c == 7))
                        k.tt(xn_[:, j, :], pb[:, 0:TW], x_[:, j, :], ALU.add, [pb, x_], [xn_])
                    k.dma(xs[:, t0:t0 + TW].rearrange("(c p) t -> p c t", p=128), xn_[:, :, :], [xn_], [xs])
                    rms_to_h(xn_, hT, sq, rstd, pv, 8, TW)
                    k.dma(hpad[:, 1 + t0:1 + t0 + TW].rearrange("(c p) t -> p c t", p=128), hT[:, :, :], [hT], [hpad])
            xcur = xs
            if KSTOP == 3:
                P.flush()
                return nc, k

            with k.phase():
                FW = 510 if T >= 512 else T
                nft = (T + FW - 1) // FW
                wst = [k.sb("wst", [128, 2048]) for i in range(2)]
                wdn = k.sb("wdn", [128, 22 * 1024], BF16)
                wd = lambda r, c0, cw: wdn[:, r * 1024 + c0: r * 1024 + c0 + cw]
                load_w(wst, wdn, wd, lambda r, c0, cw: w_dn[l, r * 128:(r + 1) * 128, c0:c0 + cw], 22, 1024)
                wj = [k.sb("wj", [128, 8, 256], BF16) for i in range(3)]
                hin = [k.sb("hin", [128, 8, 512], BF16) for i in range(2)]
                xr = k.sb("xr", [128, 8, 512])
                actT = k.sb("actT", [128, 22, 512], BF16)
                cg = k.sb("cg", [128, 512])
                cv = k.sb("cv", [128, 512])
                yo = k.sb("yo", [128, 8, 512])
                sq = k.sb("sq", [128, 8, 512], BF16)
                rstd = k.sb("rstd", [128, 512])
                last = (l == L - 1)
                nj = 0
                for f in range(nft):
                    s0 = f * FW
                    wv = min(FW, T - s0)
                    wi = wv + 2
                    h_ = hin[f % 2]
                    k.dma(h_[:, :, 0:wi], hpad[:, s0:s0 + wi].rearrange("(c p) t -> p c t", p=128), [hpad], [h_])
                    k.dma(xr[:, :, 0:wv], xs[:, s0:s0 + wv].rearrange("(c p) t -> p c t", p=128), [xs], [xr])
                    for j in range(22):
                        w_ = wj[nj % 3]
                        nj += 1
                        k.dma(w_[:, :, 0:128], wupb[:, j * 128:(j + 1) * 128].rearrange("(c p) n -> p c n", p=128), [wupb], [w_])
                        k.dma(w_[:, :, 128:256], wupb[:, DFF + j * 128:DFF + (j + 1) * 128].rearrange("(c p) n -> p c n", p=128), [wupb], [w_])
                        for half, dst in ((0, cg), (1, cv)):
                            pb = k.ps()
                            for c in range(8):
                                k.mm(pb[:, 0:wi], w_[:, c, half * 128:(half + 1) * 128], h_[:, c, 0:wi], [w_, h_], [pb],
                                     start=(c == 0), stop=(c == 7))
                            jj = 24 + half * 22 + j
                            k.ts(dst[:, 0:wv], pb[:, 0:wv], pv[:, jj:jj + 1], None, ALU.mult, None, [pb, pv], [dst])
                            k.stt(dst[:, 0:wv], pb[:, 1:1 + wv], pv[:, jj + 44:jj + 45], dst[:, 0:wv], ALU.mult, ALU.add, [pb, pv, dst], [dst])
                            k.stt(dst[:, 0:wv], pb[:, 2:2 + wv], pv[:, jj + 88:jj + 89], dst[:, 0:wv], ALU.mult, ALU.add, [pb, pv, dst], [dst])
                        k.act(cg[:, 0:wv], cg[:, 0:wv], AF.Silu, [cg], [cg])
                        k.tt(actT[:, j, 0:wv], cg[:, 0:wv], cv[:, 0:wv], ALU.mult, [cg, cv], [actT], eng="pool")
                    for j in range(8):
                        pb = k.ps()
                        for c in range(22):
                            k.mm(pb[:, 0:wv], wdn[:, c * 1024 + j * 128: c * 1024 + (j + 1) * 128], actT[:, c, 0:wv], [wdn, actT], [pb],
                                 start=(c == 0), stop=(c == 21))
                        k.tt(yo[:, j, 0:wv], pb[:, 0:wv], xr[:, j, 0:wv], ALU.add, [pb, xr], [yo])
                    if not last:
                        k.dma(xs[:, s0:s0 + wv].rearrange("(c p) t -> p c t", p=128), yo[:, :, 0:wv], [yo], [xs])
                    else:
                        rms_to_h(yo, yo, sq, rstd, pv, 16, wv)
                        k.dma(outT[:, s0:s0 + wv].rearrange("(c p) t -> p c t", p=128), yo[:, :, 0:wv], [yo], [outT])
        P.wait_all("sp", [outT.b])
        P.flush()
    return nc, k


def _consts():
    c = np.zeros((128, NCONST), np.float32)
    p = np.arange(128)[:, None]
    f = np.arange(128)[None, :]
    sc = (p // 64) == (f // 64)
    c[:, CI:CI + 128] = np.eye(128)
    c[:, CO:CO + 128] = 1.0
    A = (p <= f) & sc
    B = (p < f) & sc
    C = (p >= f) & sc
    Dm = (p > f) & sc
    c[:, CA:CA + 128] = A
    c[:, CB:CB + 128] = B
    c[:, CC:CC + 128] = C
    c[:, CD:CD + 128] = Dm
    c[:, CSC:CSC + 128] = sc
    c[:, CNSC:CNSC + 128] = ~sc
    c[:, CM0:CM0 + 128] = B
    c[:, CM0 + 128:CM0 + 256] = A
    c[:, CM1:CM1 + 128] = Dm
    c[:, CM1 + 128:CM1 + 256] = C
    half = 8
    inv_freq = (500000.0 ** (-np.arange(half, dtype=np.float32) / half)).astype(np.float32)
    for base in (0, 64):
        for r in range(16):
            c[base + r, CMISC] = inv_freq[r % 8]
            c[base + r, CMISC + 1] = -1.0 if r < 8 else 1.0
    c[:, CMISC + 2] = 1.0
    c[:, CMISC + 3] = 1024 * 1e-6
    c[:, CMISC + 4] = 1e-6
    c[:, CMISC + 5] = 64 * 1e-6
    c[:, CMISC + 6] = -0.5
    c[:, CMISC + 7] = 64 * 64e-5
    return c


def _amask():
    m = np.zeros((3, 128, 384), np.float32)
    q = np.arange(128)[:, None]
    kk = np.arange(384)[None, :]
    band = np.abs(kk - 128 - q) <= 128
    v0 = band & (kk >= 128)
    v1 = band
    v2 = band & (kk < 256)
    for i, v in enumerate((v0, v1, v2)):
        m[i] = np.where(v, 0.0, -30000.0)
    return m


def _layout(inputs, T, L):
    f32 = np.float32
    g = {k_: np.asarray(v) for k_, v in inputs.items()}
    w_in = g["w_in"][:L].astype(f32)

    def swap_block(blk):
        o = np.zeros_like(blk)
        o[..., 0:8] = blk[..., 8:16]
        o[..., 8:16] = blk[..., 0:8]
        return o

    fm = np.zeros((L, D, 1024), f32)
    for j in range(3):
        qa = w_in[:, :, j * 64:(j + 1) * 64]
        qb_ = w_in[:, :, (j + 3) * 64:(j + 4) * 64]
        fm[:, :, j * 128:j * 128 + 64] = qa
        fm[:, :, j * 128 + 64:(j + 1) * 128] = qb_
        fm[:, :, (4 + j) * 128:(4 + j) * 128 + 64] = swap_block(qa)
        fm[:, :, (4 + j) * 128 + 64:(5 + j) * 128] = swap_block(qb_)
    for g_ in range(2):
        kb_ = w_in[:, :, 384 + g_ * 64:384 + (g_ + 1) * 64]
        fm[:, :, 384 + g_ * 64:384 + (g_ + 1) * 64] = kb_
        fm[:, :, 7 * 128 + g_ * 64:7 * 128 + (g_ + 1) * 64] = swap_block(kb_)
    tm = np.ascontiguousarray(w_in[:, :, 512:N_IN])
    pvec = np.zeros((L, 128, 256), f32)
    prow = np.zeros((L, NPR), f32)
    for l in range(L):
        pvec[l, :, 0:8] = g["norm_mix"][l].reshape(8, 128).T
        pvec[l, :, 8:16] = g["norm_ffn"][l].reshape(8, 128).T
        pvec[l, :, 16:24] = g["norm_final"].reshape(8, 128).T
        fc = g["ffn_conv"][l]
        for a in range(3):
            pvec[l, :, 24 + a * 44:24 + (a + 1) * 44] = fc[a].reshape(44, 128).T
        prow[l, PR_SINK:PR_SINK + 6] = g["attn_sink"][l]
        prow[l, PR_CONV:PR_CONV + 5760] = g["dn_conv"][l].reshape(-1)
        prow[l, PR_ALOG:PR_ALOG + 12] = g["dn_a_log"][l].reshape(-1)
        prow[l, PR_DTB:PR_DTB + 12] = g["dn_dt_bias"][l].reshape(-1)
        prow[l, PR_DNN:PR_DNN + 384] = np.tile(g["dn_norm"][l], 6)
        prow[l, PR_MU0:PR_MU0 + 1024] = g["rw_mu"][l, 0]
        prow[l, PR_MU1:PR_MU1 + 1024] = g["rw_mu"][l, 1]
        prow[l, PR_W0:PR_W0 + 512] = g["rw_w0"][l].reshape(-1)
        prow[l, PR_A0:PR_A0 + 512] = g["rw_a0"][l].reshape(-1)
        prow[l, PR_KK:PR_KK + 256] = g["rw_k_k"][l]
        prow[l, PR_KA:PR_KA + 256] = g["rw_k_a"][l]
        prow[l, PR_RK:PR_RK + 256] = g["rw_r_k"][l].reshape(-1)
        prow[l, PR_LNW:PR_LNW + 256] = g["rw_ln_w"][l]
        prow[l, PR_LNB:PR_LNB + 256] = g["rw_ln_b"][l]
    prow = np.ascontiguousarray(np.broadcast_to(prow[:, None, :], (L, 128, NPR)))
    shared = {
        "w_in_fm": fm, "w_in_tm": tm,
        "w_out": np.ascontiguousarray(g["w_out"][:L].astype(f32)),
        "w_up": np.ascontiguousarray(g["ffn_w_up"][:L].astype(f32)),
        "w_down": np.ascontiguousarray(g["ffn_w_down"][:L].astype(f32)),
        "rw_wup": np.ascontiguousarray(g["rw_w_up"][:L].astype(f32)),
        "rw_aup": np.ascontiguousarray(g["rw_a_up"][:L].astype(f32)),
        "rw_gup": np.ascontiguousarray(g["rw_g_up"][:L].astype(f32)),
        "pvec": pvec, "prow": prow, "consts": _consts(), "amask": _amask(),
    }
    return shared


_CACHE = {}


def run(inputs, T, L, batches, mixers=("att", "dn", "rw")):
    key = (T, L, tuple(mixers))
    if key not in _CACHE:
        _CACHE[key] = build(T, L, mixers)[0]
    nc = _CACHE[key]
    shared = _layout(inputs, T, L)
    x = np.asarray(inputs["x"])
    pos = np.asarray(inputs["positions"])
    in_maps = []
    for b in batches:
        m = dict(shared)
        m["xT"] = np.ascontiguousarray(x[b, :T, :].T.astype(np.float32))
        m["posf"] = np.ascontiguousarray(np.broadcast_to(pos[b, :T].astype(np.float32)[None, :], (16, T)))
        in_maps.append(m)
    res = run_bass_kernel_spmd(nc, in_maps, core_ids=list(range(len(batches))))
    out = np.stack([np.ascontiguousarray(r["outT"].T) for r in res.results], axis=0)
    return out.astype(np.float32)


def kernel(**inputs):
    x = np.asarray(inputs["x"])
    B, T, _ = x.shape
    L = np.asarray(inputs["w_in"]).shape[0]
    return run(inputs, T, L, list(range(B)))
```

```python
import contextlib
import numpy as np
import concourse.bass as bass
import concourse.mybir as mybir
from concourse.bass_utils import run_bass_kernel_spmd

F32 = mybir.dt.float32
BF16 = mybir.dt.bfloat16
AF = mybir.ActivationFunctionType
ALU = mybir.AluOpType
AX = mybir.AxisListType

ENGS = ("pe", "dve", "act", "pool", "sp")

D = 1024
HD = 64
NQ = 6
DFF = 2816
N_IN = 3224
NTM = 2712
C_AV, C_DQKV, C_DZ, C_DAL, C_DBE, C_RW = 0, 128, 1280, 1664, 1676, 1688
EPS = 1e-6


class Buf:
    __slots__ = ("name", "w", "r")

    def __init__(self, name=""):
        self.name = name
        self.w = None
        self.r = {}


class Prog:
    NDMA = 24

    def __init__(self, nc):
        self.nc = nc
        self.q = {e: [] for e in ENGS}
        self.cnt = {e: 0 for e in ENGS}
        self.seen = {e: {} for e in ENGS}
        self.ndma = 0
        self.sems = {}
        self.n_instr = 0

    def _need(self, eng, tok, waits):
        if tok is None:
            return
        k, v = tok
        if self.seen[eng].get(k, 0) >= v:
            return
        if waits.get(k, 0) < v:
            waits[k] = v

    def _deps(self, eng, reads, writes):
        waits = {}
        for b in reads:
            self._need(eng, b.w, waits)
        for b in writes:
            self._need(eng, b.w, waits)
            for k, v in b.r.items():
                self._need(eng, (k, v), waits)
        for k, v in waits.items():
            self.seen[eng][k] = v
        return list(waits.items())

    def _mark(self, tok, reads, writes):
        k, v = tok
        for b in reads:
            if b.r.get(k, 0) < v:
                b.r[k] = v
        for b in writes:
            b.w = tok
            b.r = {}

    def op(self, eng, fn, reads=(), writes=(), inc=True):
        waits = self._deps(eng, reads, writes)
        if inc:
            self.cnt[eng] += 1
            tok = (eng, self.cnt[eng])
        else:
            tok = (eng, self.cnt[eng] + 1)
        self.q[eng].append((waits, fn, (eng, 1) if inc else None))
        self._mark(tok, reads, writes)
        self.n_instr += 1
        return tok

    def dma(self, eng, out, in_, reads=(), writes=(), slow=False):
        i = self.ndma
        self.ndma += 1
        key = "d%d" % (i % self.NDMA)
        val = 16 * (i // self.NDMA + 1)
        waits = dict(self._deps(eng, reads, writes))
        if val > 16 and self.seen[eng].get(key, 0) < val - 16:
            waits[key] = max(waits.get(key, 0), val - 16)
            self.seen[eng][key] = val - 16
        if slow:
            fn = lambda e, out=out, in_=in_: e.dma_start(out=out, in_=in_, allow_slow_non_contiguous=True)
        else:
            fn = lambda e, out=out, in_=in_: e.dma_start(out=out, in_=in_)
        self.q[eng].append((list(waits.items()), fn, (key, 16)))
        tok = (key, val)
        self._mark(tok, reads, writes)
        self.n_instr += 1
        return tok

    def wait_all(self, eng, bufs):
        waits = self._deps(eng, bufs, ())
        if waits:
            self.q[eng].append((waits, None, None))

    def begin(self, st):
        keys = [e for e in ENGS if e != "sp"] + ["d%d" % i for i in range(self.NDMA)]
        for k in keys:
            self.sems[k] = st.enter_context(self.nc.semaphore("s_" + k))

    def barrier(self):
        toks = [(e, self.cnt[e]) for e in ENGS if e != "sp" and self.cnt[e] > 0]
        for i in range(min(self.ndma, self.NDMA)):
            n_on = (self.ndma - 1 - i) // self.NDMA + 1
            toks.append(("d%d" % i, 16 * n_on))
        for e in ENGS:
            waits = {}
            for t in toks:
                self._need(e, t, waits)
            for k, v in waits.items():
                self.seen[e][k] = v
            if waits:
                self.q[e].append((list(waits.items()), None, None))

    def flush(self):
        nc = self.nc
        with contextlib.ExitStack() as st:
            block = st.enter_context(nc.Block())
            sems = self.sems

            def run(e, q):
                for waits, fn, inc in q:
                    for k, v in waits:
                        e.wait_ge(sems[k], v)
                    if fn is None:
                        continue
                    ins = fn(e)
                    if inc is not None:
                        ins.then_inc(sems[inc[0]], inc[1])

            @block.tensor
            def _(e):
                run(e, self.q["pe"])

            @block.vector
            def _(e):
                run(e, self.q["dve"])

            @block.scalar
            def _(e):
                run(e, self.q["act"])

            @block.gpsimd
            def _(e):
                run(e, self.q["pool"])

            @block.sync
            def _(e):
                run(e, self.q["sp"])
        self.q = {e: [] for e in ENGS}


class Tl:
    def __init__(self, t, name):
        self.t = t
        self.b = Buf(name)

    def __getitem__(self, k):
        return self.t[k]


class TlOff(Tl):
    def __init__(self, t, off, name, buf=None):
        self.t = t
        self.off = off
        self.b = buf if buf is not None else Buf(name)

    def __getitem__(self, key):
        p, c = key
        return self.t[p, c.start + self.off: c.stop + self.off]


class K:
    def __init__(self, nc, T, L):
        self.nc, self.T, self.L = nc, T, L
        self.P = Prog(nc)
        self.st = contextlib.ExitStack()
        self.cur = self.st
        self.psn = 0
        self.rr = 0
        self.dq = 0
        self.stream = None
        self.pspool = None
        self.psbpool = None

    def sb(self, name, shape, dt=F32):
        self.nalloc = getattr(self, "nalloc", 0) + 1
        nm = "%s_%d" % (name, self.nalloc)
        return Tl(self.cur.enter_context(self.nc.sbuf_tensor(nm, list(shape), dt)), nm)

    @contextlib.contextmanager
    def phase(self):
        old = self.cur
        with contextlib.ExitStack() as ph:
            self.cur = ph
            yield
            self.P.barrier()
        self.cur = old

    def dram(self, name, shape, dt=F32, kind="Internal"):
        t = self.nc.dram_tensor(name, list(shape), dt, kind=kind)
        tl = Tl(t.ap(), name)
        return tl

    def init_psum(self):
        self.banks = [Tl(self.st.enter_context(self.nc.psum_tensor("ps%d" % i, [128, 512], F32)), "ps%d" % i)
                      for i in range(7)]
        self.bankb = Tl(self.st.enter_context(self.nc.psum_tensor("psb", [128, 1024], BF16)), "psb")

    def ps(self):
        if self.pspool is not None:
            pool = self.pspool
            pool[1] += 1
            return pool[0][pool[1] % len(pool[0])]
        b = self.banks[self.psn % 7]
        self.psn += 1
        return b

    def psb(self):
        if self.psbpool is not None:
            return self.psbpool
        return self.bankb

    def halves(self):
        hb = [TlOff(self.banks[i // 2].t, (i % 2) * 256, "hb%d" % i, buf=self.banks[i // 2].b) for i in range(14)]
        bb = [TlOff(self.bankb.t, j * 128, "bb%d" % j, buf=self.bankb.b) for j in range(8)]
        return hb, bb

    def _op(self, eng, fn, r, w):
        if self.stream is not None:
            self.stream.append((0, eng, fn, r, w))
        else:
            self.P.op(eng, fn, r, w)

    def run_streams(self, bodies):
        lists = []
        for body, pool, bpool in bodies:
            self.stream = []
            self.pspool = [pool, 0] if pool is not None else None
            self.psbpool = bpool
            body()
            lists.append(self.stream)
        self.stream = None
        self.pspool = None
        self.psbpool = None
        import os
        if os.environ.get("KSEQ"):
            lists = [sum(lists, [])]
        n = max(len(l) for l in lists) if lists else 0
        for i in range(n):
            for l in lists:
                if i < len(l):
                    it = l[i]
                    if it[0] == 0:
                        self.P.op(it[1], it[2], it[3], it[4])
                    else:
                        self.P.dma(it[1], it[2], it[3], it[4], it[5], slow=it[6])

    def _bw(self, R, W):
        return [t.b for t in R], [t.b for t in W]

    def mm(self, out, lhsT, rhs, R, W, start=True, stop=True, inc=None):
        r, w = self._bw(R, W)
        inc = True
        self._op("pe", lambda e: e.matmul(out, lhsT=lhsT, rhs=rhs, start=start, stop=stop), r, w)

    def tr(self, out, in_, ident, R, W):
        r, w = self._bw(R, W)
        self._op("pe", lambda e: e.transpose(out=out, in_=in_, identity=ident), r, w)

    def act(self, out, in_, func, R, W, bias=None, scale=None, eng="act"):
        r, w = self._bw(R, W)
        kw = {}
        if bias is not None:
            kw["bias"] = bias
        if scale is not None:
            kw["scale"] = scale
        self._op("act", lambda e: e.activation(out=out, in_=in_, func=func, **kw), r, w)

    def ve(self):
        self.rr += 1
        return "dve" if self.rr % 2 else "pool"

    def tt(self, out, a, b, op, R, W, eng="dve"):
        r, w = self._bw(R, W)
        self._op(eng, lambda e: e.tensor_tensor(out=out, in0=a, in1=b, op=op), r, w)

    def ts(self, out, a, s1, s2, op0, op1, R, W, eng="dve"):
        r, w = self._bw(R, W)
        if s2 is None:
            self._op(eng, lambda e: e.tensor_scalar(out=out, in0=a, scalar1=s1, scalar2=None, op0=op0), r, w)
        else:
            self._op(eng, lambda e: e.tensor_scalar(out=out, in0=a, scalar1=s1, scalar2=s2, op0=op0, op1=op1), r, w)

    def stt(self, out, a, s, b, op0, op1, R, W, eng="dve"):
        eng = "dve"
        r, w = self._bw(R, W)
        self._op(eng, lambda e: e.scalar_tensor_tensor(out=out, in0=a, scalar=s, in1=b, op0=op0, op1=op1), r, w)

    def cp(self, out, in_, R, W, eng="dve"):
        r, w = self._bw(R, W)
        if eng == "act":
            self._op("act", lambda e: e.activation(out=out, in_=in_, func=AF.Copy), r, w)
        else:
            self._op(eng, lambda e: e.tensor_copy(out=out, in_=in_), r, w)

    def recip(self, out, in_, R, W):
        r, w = self._bw(R, W)
        self._op("dve", lambda e: e.reciprocal(out=out, in_=in_), r, w)

    def red(self, out, in_, op, R, W, eng="dve"):
        r, w = self._bw(R, W)
        self._op(eng, lambda e: e.tensor_reduce(out=out, in_=in_, axis=AX.X, op=op), r, w)

    def memset(self, out, val, W, eng="dve"):
        r, w = self._bw([], W)
        self._op(eng, lambda e: e.memset(out, val), r, w)

    def dma(self, out, in_, R, W, eng=None, slow=False):
        if eng is None:
            eng = "sp"
        r, w = self._bw(R, W)
        if self.stream is not None:
            self.stream.append((1, eng, out, in_, r, w, slow))
            return None
        return self.P.dma(eng, out, in_, r, w, slow=slow)

    def evac(self, out, in_, R, W):
        self.rr += 1
        self.cp(out, in_, R, W, eng="act" if self.rr % 2 else "dve")


CI, CO, CA, CB, CC, CD, CSC, CNSC, CM0, CM1, CMISC = 0, 128, 256, 384, 512, 640, 768, 896, 1024, 1280, 2304
NCONST = 2320
PR_SINK, PR_CONV, PR_ALOG, PR_DTB, PR_DNN = 0, 16, 5776, 5788, 5800
PR_MU0, PR_MU1, PR_W0, PR_A0, PR_KK, PR_KA, PR_RK, PR_LNW, PR_LNB, NPR = \
    6184, 7208, 8232, 8744, 9256, 9512, 9768, 10024, 10280, 10536
MAGIC = 12582912.0
TWO_PI = 6.283185307179586


def build(T, L, mixers=("att", "dn", "rw")):
    nc = bass.Bass("TRN2", target_bir_lowering=False)
    k = K(nc, T, L)
    P = k.P
    NT = T // 128
    TW = 512 if T % 512 == 0 else 256
    NW = T // TW
    SUB = TW // 128
    inp = lambda name, shape, dt=F32: k.dram(name, shape, dt, kind="ExternalInput")
    xT_in = inp("xT", [D, T])
    pos_in = inp("posf", [16, T])
    w_in_fm = inp("w_in_fm", [L, D, 1024])
    w_in_tm = inp("w_in_tm", [L, D, NTM])
    w_out = inp("w_out", [L, D, D])
    w_up = inp("w_up", [L, D, 2 * DFF])
    w_dn = inp("w_down", [L, DFF, D])
    rw_wup = inp("rw_wup", [L, 2, 64, 256])
    rw_aup = inp("rw_aup", [L, 2, 64, 256])
    rw_gup = inp("rw_gup", [L, 128, 256])
    pvec = inp("pvec", [L, 128, 256])
    prow = inp("prow", [L, 128, NPR])
    consts = inp("consts", [128, NCONST])
    amask_in = inp("amask", [3, 128, 384])
    outT = k.dram("outT", [D, T], F32, kind="ExternalOutput")
    xs = k.dram("xs", [D, T])
    ptok = k.dram("ptok", [T + 4, NTM])
    qkr = k.dram("qkr", [512, T + 256], BF16)
    mixT = k.dram("mixT", [D, T], BF16)
    hpad = k.dram("hpad", [D, T + 2], BF16)
    ropeCS = k.dram("ropeCS", [2, 128, T])
    wupb = k.dram("wupb", [D, 2 * DFF], BF16)
    ofw = k.dram("ofw", [T, 640])

    with k.st:
        P.begin(k.st)
        k.init_psum()
        cst = k.sb("cst", [128, NCONST])
        k.dma(cst[:, :], consts[:, :], [consts], [cst])
        ident = cst[:, CI:CI + 128]
        cbf = k.sb("cbf", [128, 256], BF16)
        k.cp(cbf[:, :], cst[:, 0:256], [cst], [cbf])
        identb = cbf[:, 0:128]
        onesb = cbf[:, 128:256]
        ztf = k.sb("ztf", [128, 1024])
        k.memset(ztf[:, :], 0.0, [ztf])
        ztb = ztf[:, :].bitcast(BF16)
        amask = k.sb("amask", [128, 3, 384])
        for i in range(3):
            k.dma(amask[:, i, :], amask_in[i, :, :], [amask_in], [amask])
        lnb = cst[:, CMISC + 3:CMISC + 4]

        import os
        KSTOP = int(os.environ.get("KSTOP", "99"))
        if KSTOP == -1:
            P.flush()
            return nc, k
        with k.phase():
            for c0 in range(0, NTM, 1024):
                cw = min(1024, NTM - c0)
                k.dma(ptok[0:2, c0:c0 + cw], ztf[0:2, 0:cw], [ztf], [ptok])
                k.dma(ptok[T + 2:T + 4, c0:c0 + cw], ztf[0:2, 0:cw], [ztf], [ptok])
            k.dma(qkr[384:512, 0:128], ztb[:, 0:128], [ztf], [qkr])
            k.dma(qkr[384:512, T + 128:T + 256], ztb[:, 0:128], [ztf], [qkr])
            for c in range(8):
                k.dma(hpad[c * 128:(c + 1) * 128, 0:1], ztb[:, 0:1], [ztf], [hpad], slow=True)
                k.dma(hpad[c * 128:(c + 1) * 128, T + 1:T + 2], ztb[:, 0:1], [ztf], [hpad], slow=True)
            if KSTOP == -2:
                P.flush()
                return nc, k
            RW_ = 2048 if T % 2048 == 0 else T
            ang = k.sb("ang", [128, RW_])
            rr_ = k.sb("rr", [128, RW_])
            tab = k.sb("tab", [128, RW_])
            k.memset(tab[:, :], 0.0, [tab])
            for t0 in range(0, T, RW_):
                for base in (0, 64):
                    sl = slice(base, base + 16)
                    k.dma(ang[sl, :], pos_in[:, t0:t0 + RW_], [pos_in], [ang])
                    k.ts(ang[sl, :], ang[sl, :], cst[sl, CMISC:CMISC + 1], None, ALU.mult, None, [ang, cst], [ang])
                for ti, shift, sgncol in ((0, 1.5707963267948966, CMISC + 2), (1, 0.0, CMISC + 1)):
                    for base in (0, 64):
                        sl = slice(base, base + 16)
                        k.ts(rr_[sl, :], ang[sl, :], shift, 1.0 / TWO_PI, ALU.add, ALU.mult, [ang], [rr_])
                        k.ts(rr_[sl, :], rr_[sl, :], MAGIC, None, ALU.add, None, [rr_], [rr_])
                        k.ts(rr_[sl, :], rr_[sl, :], MAGIC, None, ALU.subtract, None, [rr_], [rr_])
                        k.stt(rr_[sl, :], rr_[sl, :], -TWO_PI, ang[sl, :], ALU.mult, ALU.add, [rr_, ang], [rr_])
                        k.ts(rr_[sl, :], rr_[sl, :], shift, 3.14159, ALU.add, ALU.min, [rr_], [rr_])
                        k.ts(rr_[sl, :], rr_[sl, :], -3.14159, None, ALU.max, None, [rr_], [rr_])
                        k.act(tab[sl, :], rr_[sl, :], AF.Sin, [rr_], [tab])
                        k.ts(tab[sl, :], tab[sl, :], cst[sl, sgncol:sgncol + 1], None, ALU.mult, None, [tab, cst], [tab])
                    k.dma(ropeCS[ti, :, t0:t0 + RW_], tab[:, :], [tab], [ropeCS])

        import os
        KSTOP = int(os.environ.get("KSTOP", "99"))
        if KSTOP == 0:
            P.flush()
            return nc, k
        wsn = [0]

        def load_w(wst, dst_tl, dst_ap_fn, src_ap_fn, rows, cols):
            for r in range(rows):
                c0 = 0
                while c0 < cols:
                    cw = min(2048, cols - c0)
                    s = wst[wsn[0] % len(wst)]
                    wsn[0] += 1
                    k.dma(s[:, 0:cw], src_ap_fn(r, c0, cw), [], [s])
                    eng = ("pool", "dve", "act")[wsn[0] % 3]
                    k.cp(dst_ap_fn(r, c0, cw), s[:, 0:cw], [s], [dst_tl], eng=eng)
                    c0 += cw

        def rms_to_h(x_, hT, sq, rstd, pv, gcol, width):
            for c in range(8):
                k.tt(sq[:, c, 0:width], x_[:, c, 0:width], x_[:, c, 0:width], ALU.mult, [x_], [sq], eng="pool" if c % 2 else "dve")
            pb = k.ps()
            for c in range(8):
                k.mm(pb[:, 0:width], onesb, sq[:, c, 0:width], [cbf, sq], [pb], start=(c == 0), stop=(c == 7))
            k.act(rstd[:, 0:width], pb[:, 0:width], AF.Ln, [pb, cst], [rstd], bias=lnb, scale=1.0)
            k.act(rstd[:, 0:width], rstd[:, 0:width], AF.Exp, [rstd], [rstd], scale=-0.5)
            for c in range(8):
                k.stt(hT[:, c, 0:width], x_[:, c, 0:width], pv[:, gcol + c:gcol + c + 1], rstd[:, 0:width], ALU.mult, ALU.mult,
                      [x_, pv, rstd], [hT], eng="pool" if c % 2 else "dve")

        def invert(E, ET, PN, PT):
            k.tt(PN[0][:, 128:256], identb, E[:, :], ALU.subtract, [cbf, E], [PN[0]])
            pb = k.ps()
            k.mm(pb[:, 0:128], ET[:, :], E[:, :], [ET, E], [pb])
            k.evac(PN[0][:, 0:128], pb[:, 0:128], [pb], [PN[0]])
            pb = k.ps()
            k.mm(pb[:, 0:128], E[:, :], ET[:, :], [ET, E], [pb])
            k.evac(PT[0][:, :], pb[:, 0:128], [pb], [PT[0]])
            cur = 0
            for lev in range(1, 6):
                PNc, PTc, PNn, PTn = PN[cur], PT[cur], PN[1 - cur], PT[1 - cur]
                pb = k.ps()
                if lev < 5:
                    k.mm(pb[:, 0:256], PTc[:, :], PNc[:, 0:256], [PTc, PNc], [pb])
                    k.evac(PNn[:, 0:128], pb[:, 0:128], [pb], [PNn])
                    k.tt(PNn[:, 128:256], pb[:, 128:256], PNc[:, 128:256], ALU.add, [pb, PNc], [PNn])
                    pb2 = k.ps()
                    k.mm(pb2[:, 0:128], PNc[:, 0:128], PTc[:, :], [PNc, PTc], [pb2])
                    k.evac(PTn[:, :], pb2[:, 0:128], [pb2], [PTn])
                else:
                    k.mm(pb[:, 0:128], PTc[:, :], PNc[:, 128:256], [PTc, PNc], [pb])
                    k.tt(PNn[:, 128:256], pb[:, 0:128], PNc[:, 128:256], ALU.add, [pb, PNc], [PNn])
                cur = 1 - cur
            return PN[cur]

        def transp(dst_ap, dst_tl, src_ap, src_tl, rows, scale=None):
            pb = k.ps()
            k.tr(pb[0:rows, 0:128], src_ap, ident, [src_tl, cst], [pb])
            if scale is None:
                k.evac(dst_ap, pb[0:rows, 0:128], [pb], [dst_tl])
            else:
                k.ts(dst_ap, pb[0:rows, 0:128], scale, None, ALU.mult, None, [pb], [dst_tl])

        xcur = xT_in
        for l in range(L):
            pv = k.sb("pv", [128, 256])
            k.dma(pv[:, :], pvec[l, :, :], [pvec], [pv])
            k.ts(pv[:, 0:24], pv[:, 0:24], 32.0, None, ALU.mult, None, [pv], [pv])

            with k.phase():
                wst = [k.sb("wst", [128, 2048]) for i in range(2)]
                WIN = k.sb("WIN", [128, 8 * (1024 + NTM)], BF16)
                wfm = lambda r, c0, cw: WIN[:, r * 1024 + c0: r * 1024 + c0 + cw]
                wtm = lambda r, c0, cw: WIN[:, 8192 + r * NTM + c0: 8192 + r * NTM + c0 + cw]
                load_w(wst, WIN, wfm, lambda r, c0, cw: w_in_fm[l, r * 128:(r + 1) * 128, c0:c0 + cw], 8, 1024)
                load_w(wst, WIN, wtm, lambda r, c0, cw: w_in_tm[l, r * 128:(r + 1) * 128, c0:c0 + cw], 8, NTM)
                if KSTOP == 10:
                    P.flush()
                    return nc, k
                xt = [k.sb("xt", [128, 8, TW]) for i in range(2)]
                sq = k.sb("sq", [128, 8, TW], BF16)
                hT = k.sb("hT", [128, 8, TW], BF16)
                rstd = k.sb("rstd", [128, TW])
                fm = k.sb("fm", [128, 8, TW])
                rot = k.sb("rot", [128, 4, TW], BF16)
                rt1 = k.sb("rt1", [128, TW])
                rcs = k.sb("rcs", [128, 2, TW])
                tmst = [k.sb("tmst", [128, NTM]) for i in range(2)]
                for w in range(NW):
                    t0 = w * TW
                    x_ = xt[w % 2]
                    k.dma(x_[:, :, :], xcur[:, t0:t0 + TW].rearrange("(c p) t -> p c t", p=128), [xcur], [x_])
                    k.dma(rcs[:, :, :], ropeCS[:, :, t0:t0 + TW].rearrange("a p t -> p a t"), [ropeCS], [rcs])
                    if KSTOP == 110:
                        P.flush()
                        return nc, k
                    if KSTOP == 111:
                        for c in range(8):
                            k.tt(sq[:, c, 0:TW], x_[:, c, 0:TW], x_[:, c, 0:TW], ALU.mult, [x_], [sq], eng="pool" if c % 2 else "dve")
                        P.flush()
                        return nc, k
                    if KSTOP == 112:
                        for c in range(8):
                            k.tt(sq[:, c, 0:TW], x_[:, c, 0:TW], x_[:, c, 0:TW], ALU.mult, [x_], [sq], eng="pool" if c % 2 else "dve")
                        pb = k.ps()
                        for c in range(8):
                            k.mm(pb[:, 0:TW], onesb, sq[:, c, 0:TW], [cbf, sq], [pb], start=(c == 0), stop=(c == 7))
                        k.cp(rstd[:, 0:TW], pb[:, 0:TW], [pb], [rstd])
                        P.flush()
                        return nc, k
                    rms_to_h(x_, hT, sq, rstd, pv, 0, TW)
                    if KSTOP == 11:
                        P.flush()
                        return nc, k
                    for j in range(8):
                        pb = k.ps()
                        for c in range(8):
                            k.mm(pb[:, 0:TW], WIN[:, c * 1024 + j * 128: c * 1024 + (j + 1) * 128], hT[:, c, :], [WIN, hT], [pb],
                                 start=(c == 0), stop=(c == 7))
                        k.evac(fm[:, j, :], pb[:, 0:TW], [pb], [fm])
                    if KSTOP == 12:
                        P.flush()
                        return nc, k
                    for j in range(4):
                        k.cp(rot[:, j, :], fm[:, j, :], [fm], [rot], eng="pool")
                        for base in (0, 64):
                            sl = slice(base, base + 16)
                            k.tt(rt1[sl, :], fm[sl, j, :], rcs[sl, 0, :], ALU.mult, [fm, rcs], [rt1])
                            k.tt(fm[sl, 4 + j, :], fm[sl, 4 + j, :], rcs[sl, 1, :], ALU.mult, [fm, rcs], [fm])
                            k.tt(rot[sl, j, :], rt1[sl, :], fm[sl, 4 + j, :], ALU.add, [rt1, fm], [rot])
                        off = 128 if j == 3 else 0
                        k.dma(qkr[j * 128:(j + 1) * 128, off + t0: off + t0 + TW], rot[:, j, :], [rot], [qkr])
                    if KSTOP == 13:
                        P.flush()
                        return nc, k
                    for s in range(SUB):
                        st_ = tmst[s % 2]
                        c0 = 0
                        while c0 < NTM:
                            cw = min(512, NTM - c0)
                            pb = k.ps()
                            for c in range(8):
                                k.mm(pb[:, 0:cw], hT[:, c, s * 128:(s + 1) * 128],
                                     WIN[:, 8192 + c * NTM + c0: 8192 + c * NTM + c0 + cw], [hT, WIN], [pb],
                                     start=(c == 0), stop=(c == 7))
                            k.evac(st_[:, c0:c0 + cw], pb[:, 0:cw], [pb], [st_])
                            c0 += cw
                        k.dma(ptok[2 + t0 + s * 128: 2 + t0 + (s + 1) * 128, :], st_[:, :], [st_], [ptok])

            if KSTOP == 1:
                P.flush()
                return nc, k
            if "att" in mixers:
                with k.phase():
                    qb = [k.sb("qb", [128, 3, 128], BF16) for i in range(2)]
                    kb = [k.sb("kb", [128, 384], BF16) for i in range(2)]
                    vb = [k.sb("vb", [128, 3, 128]) for i in range(2)]
                    vbb = [k.sb("vbb", [128, 3, 128], BF16) for i in range(2)]
                    sm = k.sb("sm", [128, 384])
                    pp = k.sb("pp", [128, 384], BF16)
                    pT = k.sb("pT", [128, 384], BF16)
                    st1 = k.sb("st1", [128, 8])
                    Otok = k.sb("Otok", [128, 384])
                    OT = k.sb("OT", [128, 3, 128], BF16)
                    prA = k.sb("prA", [128, 16])
                    k.dma(prA[:, 0:6], prow[l, :, PR_SINK:PR_SINK + 6], [prow], [prA])
                    k.ts(prA[:, 8:14], prA[:, 0:6], -1.0, None, ALU.mult, None, [prA], [prA])
                    for i in range(NT):
                        q_, k_, v_, vb_ = qb[i % 2], kb[i % 2], vb[i % 2], vbb[i % 2]
                        k.dma(q_[:, :, :], qkr[0:384, i * 128:(i + 1) * 128].rearrange("(c p) t -> p c t", p=128), [qkr], [q_])
                        k.dma(k_[:, :], qkr[384:512, i * 128: i * 128 + 384], [qkr], [k_])
                        for b3 in range(3):
                            blk = i - 1 + b3
                            if 0 <= blk < NT:
                                k.dma(v_[:, b3, :], ptok[2 + blk * 128: 2 + (blk + 1) * 128, C_AV:C_AV + 128], [ptok], [v_])
                            else:
                                k.memset(v_[:, b3, :], 0.0, [v_], eng="pool")
                        k.cp(vb_[:, :, :], v_[:, :, :], [v_], [vb_], eng="pool")
                        mi = 0 if i == 0 else (2 if i == NT - 1 else 1)
                        po = k.ps()
                        for h in range(NQ):
                            g = h // 3
                            cj = h % 3
                            ps_ = k.ps()
                            k.mm(ps_[:, 0:384], q_[g * 64:(g + 1) * 64, cj, :], k_[g * 64:(g + 1) * 64, :], [q_, k_], [ps_])
                            k.tt(sm[:, :], ps_[:, 0:384], amask[:, mi, :], ALU.add, [ps_, amask], [sm])
                            k.red(st1[:, 0:1], sm[:, :], ALU.max, [sm], [st1])
                            k.ts(st1[:, 1:2], st1[:, 0:1], -0.125, prA[:, 8 + h:9 + h], ALU.mult, ALU.min, [st1, prA], [st1])
                            k.act(pp[:, :], sm[:, :], AF.Exp, [sm, st1], [pp], bias=st1[:, 1:2], scale=0.125)
                            k.red(st1[:, 2:3], pp[:, :], ALU.add, [pp], [st1])
                            k.act(st1[:, 3:4], prA[:, h:h + 1], AF.Exp, [prA, st1], [st1], bias=st1[:, 1:2], scale=1.0)
                            k.tt(st1[:, 4:5], st1[:, 2:3], st1[:, 3:4], ALU.add, [st1], [st1])
                            k.recip(st1[:, 5:6], st1[:, 4:5], [st1], [st1])
                            pbb = k.bankb
                            for b3 in range(3):
                                k.tr(pbb[:, b3 * 128:(b3 + 1) * 128], pp[:, b3 * 128:(b3 + 1) * 128], identb, [pp, cbf], [pbb])
                            k.evac(pT[:, :], pbb[:, 0:384], [pbb], [pT])
                            for b3 in range(3):
                                k.mm(po[:, h * 64:(h + 1) * 64], pT[:, b3 * 128:(b3 + 1) * 128], vb_[:, b3, g * 64:(g + 1) * 64],
                                     [pT, vb_], [po], start=(b3 == 0), stop=(b3 == 2))
                            k.ts(Otok[:, h * 64:(h + 1) * 64], po[:, h * 64:(h + 1) * 64], st1[:, 5:6], None, ALU.mult, None,
                                 [po, st1], [Otok])
                        pt_ = k.ps()
                        for c in range(3):
                            k.tr(pt_[:, c * 128:(c + 1) * 128], Otok[:, c * 128:(c + 1) * 128], ident, [Otok, cst], [pt_])
                        k.evac(OT[:, :, :], pt_[:, 0:384].rearrange("p (c t) -> p c t", c=3), [pt_], [OT])
                        k.dma(mixT[0:384, i * 128:(i + 1) * 128].rearrange("(c p) t -> p c t", p=128), OT[:, :, :], [OT], [mixT])
            zero_chunks = ([] if "att" in mixers else [0, 1, 2]) + ([] if "dn" in mixers else [3, 4, 5]) + \
                          ([] if "rw" in mixers else [6, 7])
            for c in zero_chunks:
                for t0 in range(0, T, 2048):
                    tw = min(2048, T - t0)
                    k.dma(mixT[c * 128:(c + 1) * 128, t0:t0 + tw], ztb[:, 0:tw], [ztf], [mixT])

            if "dn" in mixers:
                with k.phase():
                    hbk, bbk = k.halves()
                    cw_ = k.sb("cw", [128, 5, 1152])
                    k.dma(cw_[:, :, :], prow[l, :, PR_CONV:PR_CONV + 5760].rearrange("p (a c) -> p a c", a=5), [prow], [cw_])
                    prD = k.sb("prD", [128, 416])
                    k.dma(prD[:, 0:408], prow[l, :, PR_ALOG:PR_ALOG + 408], [prow], [prD])
                    Aexp = k.sb("Aexp", [128, 12])
                    k.act(Aexp[:, :], prD[:, 0:12], AF.Exp, [prD], [Aexp])
                    PB = [dict(xa=[k.sb("xa", [128, 1152]) for _ in range(2)], acc=k.sb("acc", [128, 1152]),
                               tmp=k.sb("tmpc", [128, 1152]), qkv=k.sb("qkv", [128, 1152]), sqv=k.sb("sqv", [128, 768]),
                               sm12=k.sb("sm12", [128, 64]), ab=k.sb("ab", [128, 24]), gcs=k.sb("gcs", [128, 64]))
                          for _ in range(2)]
                    HB = [dict(kT=k.sb("kT", [64, 128], BF16), qT=k.sb("qT", [64, 128], BF16), qgT=k.sb("qgT", [64, 128], BF16),
                               qg=k.sb("qg", [128, 64]), Gl=k.sb("Gl", [128, 128]), DE=k.sb("DE", [128, 128]),
                               E1=k.sb("E1", [128, 128]), E=k.sb("E", [128, 128], BF16), ET=k.sb("ET", [128, 128], BF16),
                               LY=k.sb("LY", [128, 128], BF16), PN=[k.sb("PN", [128, 256], BF16) for _ in range(2)],
                               PT=[k.sb("PT", [128, 128], BF16) for _ in range(2)], VK=k.sb("VK", [128, 128], BF16),
                               UVb=k.sb("UVb", [128, 64]), LA=k.sb("LA", [64, 128], BF16), LB=k.sb("LB", [128, 64], BF16),
                               vnb=k.sb("vnb", [128, 64], BF16), S=k.sb("S", [64, 64]), Sb=k.sb("Sb", [64, 64], BF16),
                               oh=[k.sb("oh", [128, 64]) for _ in range(2)])
                          for _ in range(6)]
                    QB = dict(otile=k.sb("otile", [128, 384]), oprev=k.sb("oprev", [128, 384]), zt_=k.sb("zt_", [128, 384]),
                              osq=k.sb("osq", [128, 384]), oT=k.sb("oT", [128, 3, 128], BF16), smp=k.sb("smp", [128, 32]))

                    def dn_prep(i, B, d):
                        Ucol = CA if d == 0 else CC
                        r0 = 2 + i * 128
                        acc, tmp, qkv, sqv, sm12, ab, gcs = B["acc"], B["tmp"], B["qkv"], B["sqv"], B["sm12"], B["ab"], B["gcs"]
                        for a in range(5):
                            x_ = B["xa"][a % 2]
                            k.dma(x_[:, :], ptok[r0 + a - 2: r0 + a - 2 + 128, C_DQKV:C_DQKV + 1152], [ptok], [x_])
                            if a == 0:
                                k.tt(acc[:, :], x_[:, :], cw_[:, 0, :], ALU.mult, [x_, cw_], [acc])
                            else:
                                k.tt(tmp[:, :], x_[:, :], cw_[:, a, :], ALU.mult, [x_, cw_], [tmp], eng="pool")
                                k.tt(acc[:, :], acc[:, :], tmp[:, :], ALU.add, [acc, tmp], [acc], eng="pool")
                        k.act(qkv[:, :], acc[:, :], AF.Silu, [acc], [qkv])
                        k.tt(sqv[:, :], qkv[:, 0:768], qkv[:, 0:768], ALU.mult, [qkv], [sqv], eng="pool")
                        for hh in range(12):
                            k.red(sm12[:, hh:hh + 1], sqv[:, hh * 64:(hh + 1) * 64], ALU.add, [sqv], [sm12])
                        k.act(sm12[:, 12:24], sm12[:, 0:12], AF.Ln, [sm12, cst], [sm12], bias=cst[:, CMISC + 4:CMISC + 5], scale=1.0)
                        k.act(sm12[:, 12:24], sm12[:, 12:24], AF.Exp, [sm12], [sm12], scale=-0.5)
                        k.ts(sm12[:, 12:18], sm12[:, 12:18], 0.125, None, ALU.mult, None, [sm12], [sm12])
                        for hh in range(12):
                            k.ts(qkv[:, hh * 64:(hh + 1) * 64], qkv[:, hh * 64:(hh + 1) * 64], sm12[:, 12 + hh:13 + hh], None,
                                 ALU.mult, None, [qkv, sm12], [qkv], eng="pool")
                        k.dma(ab[:, :], ptok[r0:r0 + 128, C_DAL:C_DAL + 24], [ptok], [ab])
                        k.tt(sm12[:, 24:36], ab[:, 0:12], prD[:, 12:24], ALU.add, [ab, prD], [sm12])
                        k.act(sm12[:, 24:36], sm12[:, 24:36], AF.Exp, [sm12], [sm12])
                        k.act(sm12[:, 24:36], sm12[:, 24:36], AF.Ln, [sm12], [sm12], bias=1.0, scale=1.0)
                        k.stt(sm12[:, 24:36], sm12[:, 24:36], -1.0, Aexp[:, :], ALU.mult, ALU.mult, [sm12, Aexp], [sm12])
                        k.act(sm12[:, 36:48], ab[:, 12:24], AF.Sigmoid, [ab], [sm12])
                        gsl = sm12[:, 24 + d * 6: 30 + d * 6]
                        pb = k.ps()
                        k.mm(pb[:, 0:6], cst[:, Ucol:Ucol + 128], gsl, [cst, sm12], [pb])
                        k.mm(pb[:, 8:14], cst[:, CSC:CSC + 128], gsl, [cst, sm12], [pb])
                        k.mm(pb[:, 16:22], cst[:, CNSC:CNSC + 128], gsl, [cst, sm12], [pb])
                        k.cp(gcs[:, 0:24], pb[:, 0:24], [pb], [gcs])
                        k.act(gcs[:, 24:30], gcs[:, 0:6], AF.Exp, [gcs], [gcs])
                        k.tt(gcs[:, 32:38], gcs[:, 8:14], gcs[:, 0:6], ALU.subtract, [gcs], [gcs])
                        k.act(gcs[:, 32:38], gcs[:, 32:38], AF.Exp, [gcs], [gcs])
                        k.act(gcs[:, 40:46], gcs[:, 8:14], AF.Exp, [gcs], [gcs])
                        k.act(gcs[:, 48:54], gcs[:, 16:22], AF.Exp, [gcs], [gcs])

                    def dn_head(i, h, B, H, d, par):
                        Ucol = CA if d == 0 else CC
                        Gcol = CD if d == 0 else CB
                        Mcol = CM0 if d == 0 else CM1
                        qkv, sm12, gcs = B["qkv"], B["sm12"], B["gcs"]
                        kT, qT, qgT, qg, Gl, DE, E1, E, ET, LY = (H[n_] for n_ in ("kT", "qT", "qgT", "qg", "Gl", "DE", "E1", "E", "ET", "LY"))
                        VK, UVb, LA, LB, vnb, S, Sb = (H[n_] for n_ in ("VK", "UVb", "LA", "LB", "vnb", "S", "Sb"))
                        oh = H["oh"][par]
                        qh = qkv[:, h * 64:(h + 1) * 64]
                        kh = qkv[:, 384 + h * 64: 384 + (h + 1) * 64]
                        vh = qkv[:, 768 + h * 64: 768 + (h + 1) * 64]
                        bcol = sm12[:, 36 + d * 6 + h: 37 + d * 6 + h]
                        wcol = gcs[:, 24 + h:25 + h]
                        transp(kT[:, :], kT, kh, qkv, 64)
                        transp(qT[:, :], qT, qh, qkv, 64)
                        k.ts(qg[:, :], qh, wcol, None, ALU.mult, None, [qkv, gcs], [qg])
                        transp(qgT[:, :], qgT, qg[:, :], qg, 64)
                        k.ts(Gl[:, :], cst[:, Gcol:Gcol + 128], sm12[:, 24 + d * 6 + h:25 + d * 6 + h], None, ALU.mult, None,
                             [cst, sm12], [Gl], eng="pool")
                        pbd = k.ps()
                        k.mm(pbd[:, 0:128], Gl[:, :], cst[:, Ucol:Ucol + 128], [Gl, cst], [pbd])
                        k.act(DE[:, :], pbd[:, 0:128], AF.Exp, [pbd], [DE])
                        pbk = k.ps()
                        k.mm(pbk[:, 0:128], kT[:, :], kT[:, :], [kT], [pbk])
                        k.mm(pbk[:, 128:256], kT[:, :], qT[:, :], [kT, qT], [pbk])
                        k.tt(E1[:, :], pbk[:, 0:128], DE[:, :], ALU.mult, [pbk, DE], [E1])
                        k.stt(E[:, :], E1[:, :], bcol, cst[:, Mcol:Mcol + 128], ALU.mult, ALU.mult, [E1, sm12, cst], [E])
                        k.tt(E1[:, :], pbk[:, 128:256], DE[:, :], ALU.mult, [pbk, DE], [E1])
                        k.tt(LY[:, :], E1[:, :], cst[:, Mcol + 128:Mcol + 256], ALU.mult, [E1, cst], [LY], eng="pool")
                        pbb = k.psb()
                        k.tr(pbb[:, 0:128], E[:, :], identb, [E, cbf], [pbb])
                        k.evac(ET[:, :], pbb[:, 0:128], [pbb], [ET])
                        Nt = invert(E, ET, H["PN"], H["PT"])
                        N_ = Nt[:, 128:256]
                        k.cp(VK[:, 0:64], vh, [qkv], [VK], eng="pool")
                        k.ts(VK[:, 64:128], kh, wcol, None, ALU.mult, None, [qkv, gcs], [VK])
                        k.ts(LB[:, :], kh, gcs[:, 32 + h:33 + h], None, ALU.mult, None, [qkv, gcs], [LB], eng="pool")
                        pbv = k.ps()
                        k.mm(pbv[:, 0:64], N_, VK[:, 0:64], [Nt, VK], [pbv])
                        k.ts(UVb[:, :], pbv[:, 0:64], bcol, None, ALU.mult, None, [pbv, sm12], [UVb])
                        pbl = k.ps()
                        k.mm(pbl[0:64, 0:128], VK[:, 64:128], N_, [Nt, VK], [pbl])
                        k.ts(LA[:, :], pbl[0:64, 0:128], -1.0, None, ALU.mult, None, [pbl], [LA])
                        for ci in ((0, 1) if d == 0 else (1, 0)):
                            rs = slice(ci * 64, ci * 64 + 64)
                            glc = gcs[0:64, 40 + h:41 + h] if ci == 0 else gcs[0:64, 48 + h:49 + h]
                            pa = k.ps()
                            k.mm(pa[:, 0:64], LA[:, :], Sb[:, :], [LA, Sb], [pa])
                            k.stt(vnb[rs, :], pa[rs, 0:64], sm12[rs, 36 + d * 6 + h: 37 + d * 6 + h], UVb[rs, :], ALU.mult, ALU.add,
                                  [pa, sm12, UVb], [vnb])
                            po_ = k.ps()
                            k.mm(po_[:, 0:64], qgT[:, :], Sb[:, :], [qgT, Sb], [po_], start=True, stop=False)
                            k.mm(po_[:, 0:64], LY[rs, :], vnb[rs, :], [LY, vnb], [po_], start=False, stop=True)
                            k.cp(oh[rs, :], po_[rs, 0:64], [po_], [oh], eng="act")
                            k.mm(pa[0:64, 128:192], LB[rs, :], vnb[rs, :], [LB, vnb], [pa])
                            k.stt(S[:, :], S[:, :], glc, pa[0:64, 128:192], ALU.mult, ALU.add, [S, gcs, pa], [S])
                            k.cp(Sb[:, :], S[:, :], [S], [Sb], eng="pool")

                    def dn_post(i, d, par):
                        otile, oprev, zt_, osq, oT, smp = (QB[n_] for n_ in ("otile", "oprev", "zt_", "osq", "oT", "smp"))
                        r0 = 2 + i * 128
                        for h in range(6):
                            k.cp(otile[:, h * 64:(h + 1) * 64], HB[h]["oh"][par][:, :], [HB[h]["oh"][par]], [otile], eng="pool")
                        if d == 0:
                            k.dma(ofw[i * 128:(i + 1) * 128, 0:384], otile[:, :], [otile], [ofw])
                            return
                        k.dma(oprev[:, :], ofw[i * 128:(i + 1) * 128, 0:384], [ofw], [oprev])
                        k.dma(zt_[:, :], ptok[r0:r0 + 128, C_DZ:C_DZ + 384], [ptok], [zt_])
                        k.tt(otile[:, :], otile[:, :], oprev[:, :], ALU.add, [otile, oprev], [otile])
                        k.tt(osq[:, :], otile[:, :], otile[:, :], ALU.mult, [otile], [osq], eng="pool")
                        for h in range(6):
                            k.red(smp[:, h:h + 1], osq[:, h * 64:(h + 1) * 64], ALU.add, [osq], [smp])
                        k.act(smp[:, 8:14], smp[:, 0:6], AF.Ln, [smp, cst], [smp], bias=cst[:, CMISC + 5:CMISC + 6], scale=1.0)
                        k.act(smp[:, 8:14], smp[:, 8:14], AF.Exp, [smp], [smp], scale=-0.5)
                        k.ts(smp[:, 8:14], smp[:, 8:14], 8.0, None, ALU.mult, None, [smp], [smp])
                        for h in range(6):
                            k.ts(otile[:, h * 64:(h + 1) * 64], otile[:, h * 64:(h + 1) * 64], smp[:, 8 + h:9 + h], None,
                                 ALU.mult, None, [otile, smp], [otile])
                        k.tt(otile[:, :], otile[:, :], prD[:, 24:408], ALU.mult, [otile, prD], [otile])
                        k.act(osq[:, :], zt_[:, :], AF.Silu, [zt_], [osq])
                        k.tt(otile[:, :], otile[:, :], osq[:, :], ALU.mult, [otile, osq], [otile])
                        for c in range(3):
                            pt_ = k.ps()
                            k.tr(pt_[:, 0:128], otile[:, c * 128:(c + 1) * 128], ident, [otile, cst], [pt_])
                            k.evac(oT[:, c, :], pt_[:, 0:128], [pt_], [oT])
                        k.dma(mixT[384:768, i * 128:(i + 1) * 128].rearrange("(c p) t -> p c t", p=128), oT[:, :, :], [oT], [mixT])

                    for d in (0, 1):
                        for h in range(6):
                            k.memset(HB[h]["S"][:, :], 0.0, [HB[h]["S"]])
                            k.memset(HB[h]["Sb"][:, :], 0.0, [HB[h]["Sb"]], eng="pool")
                        order = list(range(NT)) if d == 0 else list(range(NT - 1, -1, -1))
                        k.run_streams([((lambda i=order[0]: dn_prep(i, PB[0], d)), [hbk[12]], None)])
                        for n, i in enumerate(order):
                            bodies = [((lambda i=i, h=h, n=n: dn_head(i, h, PB[n % 2], HB[h], d, n % 2)), [k.banks[h]], bbk[h])
                                      for h in range(6)]
                            KDN = int(os.environ.get("KDN", "3"))
                            if KDN < 2:
                                bodies = []
                            if n + 1 < len(order):
                                bodies.append(((lambda j=order[n + 1], n=n: dn_prep(j, PB[(n + 1) % 2], d)), [hbk[12]], None))
                            if n >= 1 and KDN >= 3:
                                bodies.append(((lambda j=order[n - 1], n=n: dn_post(j, d, (n - 1) % 2)), [hbk[13]], None))
                            k.run_streams(bodies)
                        if KDN >= 3:
                            k.run_streams([((lambda j=order[-1]: dn_post(j, d, (len(order) - 1) % 2)), [hbk[13]], None)])

            if "rw" in mixers:
                with k.phase():
                    hbk, bbk = k.halves()
                    prR = k.sb("prR", [128, NPR - PR_MU0])
                    k.dma(prR[:, :], prow[l, :, PR_MU0:NPR], [prow], [prR])
                    o_ = lambda off: off - PR_MU0
                    omka = k.sb("omka", [128, 256])
                    k.ts(omka[:, :], prR[:, o_(PR_KA):o_(PR_KA) + 256], -1.0, 1.0, ALU.mult, ALU.add, [prR], [omka])
                    wtmp = k.sb("wtmp", [128, 256])
                    wupb_ = k.sb("wupb_", [64, 2, 256], BF16)
                    aupP = k.sb("aupP", [128, 2, 256], BF16)
                    gupb_ = k.sb("gupb_", [128, 256], BF16)
                    for dd in range(2):
                        k.dma(wtmp[0:64, :], rw_wup[l, dd, :, :], [rw_wup], [wtmp])
                        k.cp(wupb_[:, dd, :], wtmp[0:64, :], [wtmp], [wupb_])
                        k.dma(wtmp[64:128, :], rw_aup[l, dd, :, :], [rw_aup], [wtmp])
                        k.cp(aupP[64:128, dd, :], wtmp[64:128, :], [wtmp], [aupP])
                    k.dma(wtmp[:, :], rw_gup[l, :, :], [rw_gup], [wtmp])
                    k.cp(gupb_[:, :], wtmp[:, :], [wtmp], [gupb_])
                    f256 = ("kkn", "gate", "lw", "av", "kd", "bb", "Linc", "Lexc", "Ltot", "Loth", "at_", "bt_", "kt_", "rt_", "wct", "wco", "ysq")
                    b256 = ("Bh", "Kh", "atb", "vbf")
                    PB = []
                    for _ in range(3):
                        dct = dict(pc=k.sb("pc", [128, 1024]), pp_=k.sb("pprev", [128, 1024]), pn_=k.sb("pnext", [128, 1024]),
                                   sm8=k.sb("sm8", [128, 32]), tT=k.sb("tT", [128, 128], BF16))
                        for n_ in f256:
                            dct[n_] = k.sb(n_, [128, 256])
                        for n_ in b256:
                            dct[n_] = k.sb(n_, [128, 256], BF16)
                        PB.append(dct)
                    HB = [dict(aT=k.sb("aT", [64, 256], BF16), bT=k.sb("bT", [64, 128], BF16), kT2=k.sb("kT2", [64, 128], BF16),
                               wcT=k.sb("wcT", [64, 256]), E=k.sb("Er", [128, 128], BF16), ET=k.sb("ETr", [128, 128], BF16),
                               NRB=k.sb("NRB", [128, 128], BF16), AK=k.sb("AK", [128, 128], BF16), RK=k.sb("RK", [128, 128], BF16),
                               PN=[k.sb("PNr", [128, 256], BF16) for _ in range(2)], PT=[k.sb("PTr", [128, 128], BF16) for _ in range(2)],
                               Xb=k.sb("Xb", [128, 64], BF16), UV=k.sb("UVr", [128, 64]), LA=k.sb("LAr", [64, 128], BF16),
                               Ub=k.sb("Ub", [128, 64], BF16), H=k.sb("H", [64, 64]), Hb=k.sb("Hb", [64, 64], BF16),
                               yh=[k.sb("yh", [128, 64]) for _ in range(2)])
                          for _ in range(4)]
                    QB = dict(ytile=k.sb("ytile", [128, 256]), yprev=k.sb("yprev", [128, 256]), ysq=k.sb("ysq2", [128, 256]),
                              yT=k.sb("yT", [128, 2, 128], BF16), sm8=k.sb("sm8q", [128, 32]))

                    def rw_prep(i, B, d):
                        Ucol = CA if d == 0 else CC
                        Xcol = CB if d == 0 else CD
                        r0 = 2 + i * 128
                        pc, pp_, pn_, sm8, tT = B["pc"], B["pp_"], B["pn_"], B["sm8"], B["tT"]
                        kkn, gate, lw, av, kd, bb, Linc, Lexc, Ltot, Loth, at_, bt_, kt_, rt_, wct, wco, ysq = (B[n_] for n_ in f256)
                        Bh, Kh, atb, vbf = (B[n_] for n_ in b256)
                        k.dma(pc[:, :], ptok[r0:r0 + 128, C_RW:C_RW + 1024], [ptok], [pc])
                        k.dma(pp_[:, :], ptok[r0 - 1:r0 + 127, C_RW:C_RW + 1024], [ptok], [pp_])
                        k.dma(pn_[:, :], ptok[r0 + 1:r0 + 129, C_RW:C_RW + 1024], [ptok], [pn_])
                        k.tt(pp_[:, :], pp_[:, :], pc[:, :], ALU.subtract, [pp_, pc], [pp_])
                        k.tt(pn_[:, :], pn_[:, :], pc[:, :], ALU.subtract, [pn_, pc], [pn_], eng="pool")
                        k.tt(pp_[:, :], pp_[:, :], prR[:, o_(PR_MU0):o_(PR_MU0) + 1024], ALU.mult, [pp_, prR], [pp_])
                        k.tt(pn_[:, :], pn_[:, :], prR[:, o_(PR_MU1):o_(PR_MU1) + 1024], ALU.mult, [pn_, prR], [pn_], eng="pool")
                        k.tt(pc[:, :], pc[:, :], pp_[:, :], ALU.add, [pc, pp_], [pc])
                        k.tt(pc[:, :], pc[:, :], pn_[:, :], ALU.add, [pc, pn_], [pc], eng="pool")
                        r_, k_, v_ = pc[:, 0:256], pc[:, 256:512], pc[:, 512:768]
                        k.act(pc[:, 768:832], pc[:, 768:832], AF.Tanh, [pc], [pc])
                        k.act(pc[:, 896:1024], pc[:, 896:1024], AF.Sigmoid, [pc], [pc])
                        transp(tT[:, :], tT, pc[:, 896:1024], pc, 128)
                        pb = k.ps()
                        k.mm(pb[:, 0:256], tT[:, :], gupb_[:, :], [tT, gupb_], [pb])
                        k.cp(gate[:, :], pb[:, 0:256], [pb], [gate], eng="act")
                        transp(tT[:, :], tT, pc[:, 768:896], pc, 128)
                        pb = k.ps()
                        k.mm(pb[:, 0:256], tT[0:64, :], wupb_[:, d, :], [tT, wupb_], [pb])
                        k.tt(lw[:, :], pb[:, 0:256], prR[:, o_(PR_W0) + d * 256:o_(PR_W0) + (d + 1) * 256], ALU.add, [pb, prR], [lw])
                        pb2 = k.ps()
                        k.mm(pb2[:, 0:256], tT[64:128, :], aupP[64:128, d, :], [tT, aupP], [pb2])
                        k.tt(av[:, :], pb2[:, 0:256], prR[:, o_(PR_A0) + d * 256:o_(PR_A0) + (d + 1) * 256], ALU.add, [pb2, prR], [av])
                        k.act(av[:, :], av[:, :], AF.Sigmoid, [av], [av])
                        k.act(lw[:, :], lw[:, :], AF.Exp, [lw], [lw], scale=-1.0)
                        k.act(lw[:, :], lw[:, :], AF.Ln, [lw], [lw], bias=1.0, scale=1.0)
                        k.act(lw[:, :], lw[:, :], AF.Exp, [lw, cst], [lw], bias=cst[:, CMISC + 6:CMISC + 7], scale=-1.0)
                        k.ts(lw[:, :], lw[:, :], -1.0, None, ALU.mult, None, [lw], [lw], eng="pool")
                        k.tt(kkn[:, :], k_, prR[:, o_(PR_KK):o_(PR_KK) + 256], ALU.mult, [pc, prR], [kkn], eng="pool")
                        k.tt(ysq[:, :], kkn[:, :], kkn[:, :], ALU.mult, [kkn], [ysq], eng="pool")
                        for h in range(4):
                            k.red(sm8[:, h:h + 1], ysq[:, h * 64:(h + 1) * 64], ALU.add, [ysq], [sm8])
                        k.act(sm8[:, 4:8], sm8[:, 0:4], AF.Ln, [sm8, cst], [sm8], bias=cst[:, CMISC + 4:CMISC + 5], scale=1.0)
                        k.act(sm8[:, 4:8], sm8[:, 4:8], AF.Exp, [sm8], [sm8], scale=-0.5)
                        for h in range(4):
                            k.ts(kkn[:, h * 64:(h + 1) * 64], kkn[:, h * 64:(h + 1) * 64], sm8[:, 4 + h:5 + h], None, ALU.mult, None,
                                 [kkn, sm8], [kkn], eng="pool")
                        k.tt(kd[:, :], av[:, :], prR[:, o_(PR_KA):o_(PR_KA) + 256], ALU.mult, [av, prR], [kd], eng="pool")
                        k.tt(kd[:, :], kd[:, :], omka[:, :], ALU.add, [kd, omka], [kd], eng="pool")
                        k.tt(kd[:, :], kd[:, :], k_, ALU.mult, [kd, pc], [kd], eng="pool")
                        k.tt(bb[:, :], kkn[:, :], av[:, :], ALU.mult, [kkn, av], [bb], eng="pool")
                        for (dst, col) in ((Linc, Ucol), (Lexc, Xcol), (Ltot, CSC), (Loth, CNSC)):
                            pb = k.ps()
                            k.mm(pb[:, 0:256], cst[:, col:col + 128], lw[:, :], [cst, lw], [pb])
                            k.evac(dst[:, :], pb[:, 0:256], [pb], [dst])
                        k.act(wct[:, :], Ltot[:, :], AF.Exp, [Ltot], [wct])
                        k.act(wco[:, :], Loth[:, :], AF.Exp, [Loth], [wco])
                        k.tt(Ltot[:, :], Ltot[:, :], Linc[:, :], ALU.subtract, [Ltot, Linc], [Ltot], eng="pool")
                        k.act(Ltot[:, :], Ltot[:, :], AF.Exp, [Ltot], [Ltot])
                        k.act(Lexc[:, :], Lexc[:, :], AF.Exp, [Lexc], [Lexc])
                        k.act(Loth[:, :], Linc[:, :], AF.Exp, [Linc], [Loth], scale=-1.0)
                        k.act(Linc[:, :], Linc[:, :], AF.Exp, [Linc], [Linc])
                        k.tt(at_[:, :], kkn[:, :], Lexc[:, :], ALU.mult, [kkn, Lexc], [at_], eng="pool")
                        k.tt(bt_[:, :], bb[:, :], Loth[:, :], ALU.mult, [bb, Loth], [bt_], eng="pool")
                        k.tt(kt_[:, :], kd[:, :], Loth[:, :], ALU.mult, [kd, Loth], [kt_], eng="pool")
                        k.tt(rt_[:, :], r_, Linc[:, :], ALU.mult, [pc, Linc], [rt_], eng="pool")
                        k.stt(Bh[:, :], bb[:, :], -1.0, Ltot[:, :], ALU.mult, ALU.mult, [bb, Ltot], [Bh])
                        k.tt(Kh[:, :], kd[:, :], Ltot[:, :], ALU.mult, [kd, Ltot], [Kh], eng="pool")
                        k.cp(atb[:, :], at_[:, :], [at_], [atb], eng="pool")
                        k.cp(vbf[:, :], v_, [pc], [vbf], eng="pool")

                    def rw_head(i, h, B, Hd, d, par):
                        Mcol = CM0 if d == 0 else CM1
                        at_, bt_, kt_, rt_, wct, wco = (B[n_] for n_ in ("at_", "bt_", "kt_", "rt_", "wct", "wco"))
                        Bh, Kh, atb, vbf = (B[n_] for n_ in b256)
                        aT, bT, kT2, wcT, E, ET, NRB, AK, RK, Xb, UV, LA, Ub, H, Hb = (Hd[n_] for n_ in (
                            "aT", "bT", "kT2", "wcT", "E", "ET", "NRB", "AK", "RK", "Xb", "UV", "LA", "Ub", "H", "Hb"))
                        yh = Hd["yh"][par]
                        hs = slice(h * 64, (h + 1) * 64)
                        transp(aT[:, 0:128], aT, at_[:, hs], at_, 64)
                        transp(aT[:, 128:256], aT, rt_[:, hs], rt_, 64)
                        transp(bT[:, :], bT, bt_[:, hs], bt_, 64)
                        transp(kT2[:, :], kT2, kt_[:, hs], kt_, 64)
                        transp(wcT[:, 0:128], wcT, wct[:, hs], wct, 64)
                        transp(wcT[:, 128:256], wcT, wco[:, hs], wco, 64)
                        pb = k.ps()
                        k.mm(pb[:, 0:256], bT[:, :], aT[:, :], [bT, aT], [pb])
                        k.tt(E[:, :], pb[:, 0:128], cst[:, Mcol:Mcol + 128], ALU.mult, [pb, cst], [E])
                        k.stt(NRB[:, :], pb[:, 128:256], -1.0, cst[:, Mcol + 128:Mcol + 256], ALU.mult, ALU.mult, [pb, cst], [NRB])
                        pb = k.ps()
                        k.mm(pb[:, 0:256], kT2[:, :], aT[:, :], [kT2, aT], [pb])
                        k.tt(AK[:, :], pb[:, 0:128], cst[:, Mcol:Mcol + 128], ALU.mult, [pb, cst], [AK])
                        k.tt(RK[:, :], pb[:, 128:256], cst[:, Mcol + 128:Mcol + 256], ALU.mult, [pb, cst], [RK])
                        pbb = k.psb()
                        k.tr(pbb[:, 0:128], E[:, :], identb, [E, cbf], [pbb])
                        k.evac(ET[:, :], pbb[:, 0:128], [pbb], [ET])
                        Nt = invert(E, ET, Hd["PN"], Hd["PT"])
                        N_ = Nt[:, 128:256]
                        pb = k.ps()
                        k.mm(pb[:, 0:64], AK[:, :], vbf[:, hs], [AK, vbf], [pb])
                        k.evac(Xb[:, :], pb[:, 0:64], [pb], [Xb])
                        pb = k.ps()
                        k.mm(pb[:, 0:64], N_, Xb[:, :], [Nt, Xb], [pb])
                        k.evac(UV[:, :], pb[:, 0:64], [pb], [UV])
                        pb = k.ps()
                        k.mm(pb[0:64, 0:128], atb[:, hs], N_, [atb, Nt], [pb])
                        k.evac(LA[:, :], pb[0:64, 0:128], [pb], [LA])
                        for ci in ((0, 1) if d == 0 else (1, 0)):
                            rs = slice(ci * 64, ci * 64 + 64)
                            wcc = wcT[:, 0:1] if ci == 0 else wcT[:, 128:129]
                            pa = k.ps()
                            k.mm(pa[:, 0:64], LA[:, :], Hb[:, :], [LA, Hb], [pa])
                            k.tt(Ub[rs, :], pa[rs, 0:64], UV[rs, :], ALU.add, [pa, UV], [Ub])
                            po_ = k.ps()
                            k.mm(po_[:, 0:64], aT[:, 128:256], Hb[:, :], [aT, Hb], [po_], start=True, stop=False)
                            k.mm(po_[:, 0:64], NRB[rs, :], Ub[rs, :], [NRB, Ub], [po_], start=False, stop=False)
                            k.mm(po_[:, 0:64], RK[rs, :], vbf[rs, hs], [RK, vbf], [po_], start=False, stop=True)
                            k.cp(yh[rs, :], po_[rs, 0:64], [po_], [yh], eng="act")
                            k.mm(pa[0:64, 128:192], Bh[rs, hs], Ub[rs, :], [Bh, Ub], [pa], start=True, stop=False)
                            k.mm(pa[0:64, 128:192], Kh[rs, hs], vbf[rs, hs], [Kh, vbf], [pa], start=False, stop=True)
                            k.stt(H[:, :], H[:, :], wcc, pa[0:64, 128:192], ALU.mult, ALU.add, [H, wcT, pa], [H])
                            k.cp(Hb[:, :], H[:, :], [H], [Hb], eng="pool")

                    def rw_post(i, B, d, par):
                        ytile, yprev, ysq, yT, sm8 = (QB[n_] for n_ in ("ytile", "yprev", "ysq", "yT", "sm8"))
                        pc, gate = B["pc"], B["gate"]
                        for h in range(4):
                            k.cp(ytile[:, h * 64:(h + 1) * 64], HB[h]["yh"][par][:, :], [HB[h]["yh"][par]], [ytile], eng="pool")
                        if d == 0:
                            k.dma(ofw[i * 128:(i + 1) * 128, 384:640], ytile[:, :], [ytile], [ofw])
                            return
                        k.dma(yprev[:, :], ofw[i * 128:(i + 1) * 128, 384:640], [ofw], [yprev])
                        k.tt(ytile[:, :], ytile[:, :], yprev[:, :], ALU.add, [ytile, yprev], [ytile])
                        for h in range(4):
                            k.red(sm8[:, 8 + h:9 + h], ytile[:, h * 64:(h + 1) * 64], ALU.add, [ytile], [sm8])
                        k.ts(sm8[:, 8:12], sm8[:, 8:12], -1.0 / 64, None, ALU.mult, None, [sm8], [sm8])
                        for h in range(4):
                            k.ts(ytile[:, h * 64:(h + 1) * 64], ytile[:, h * 64:(h + 1) * 64], sm8[:, 8 + h:9 + h], None, ALU.add, None,
                                 [ytile, sm8], [ytile])
                        k.tt(ysq[:, :], ytile[:, :], ytile[:, :], ALU.mult, [ytile], [ysq], eng="pool")
                        for h in range(4):
                            k.red(sm8[:, 12 + h:13 + h], ysq[:, h * 64:(h + 1) * 64], ALU.add, [ysq], [sm8])
                        k.act(sm8[:, 12:16], sm8[:, 12:16], AF.Ln, [sm8, cst], [sm8], bias=cst[:, CMISC + 7:CMISC + 8], scale=1.0)
                        k.act(sm8[:, 12:16], sm8[:, 12:16], AF.Exp, [sm8], [sm8], scale=-0.5)
                        k.ts(sm8[:, 12:16], sm8[:, 12:16], 8.0, None, ALU.mult, None, [sm8], [sm8])
                        for h in range(4):
                            k.ts(ytile[:, h * 64:(h + 1) * 64], ytile[:, h * 64:(h + 1) * 64], sm8[:, 12 + h:13 + h], None, ALU.mult, None,
                                 [ytile, sm8], [ytile])
                        k.tt(ytile[:, :], ytile[:, :], prR[:, o_(PR_LNW):o_(PR_LNW) + 256], ALU.mult, [ytile, prR], [ytile])
                        k.tt(ytile[:, :], ytile[:, :], prR[:, o_(PR_LNB):o_(PR_LNB) + 256], ALU.add, [ytile, prR], [ytile])
                        k.tt(ysq[:, :], pc[:, 0:256], pc[:, 256:512], ALU.mult, [pc], [ysq], eng="pool")
                        k.tt(ysq[:, :], ysq[:, :], prR[:, o_(PR_RK):o_(PR_RK) + 256], ALU.mult, [ysq, prR], [ysq], eng="pool")
                        for h in range(4):
                            k.red(sm8[:, 16 + h:17 + h], ysq[:, h * 64:(h + 1) * 64], ALU.add, [ysq], [sm8])
                        for h in range(4):
                            k.stt(ytile[:, h * 64:(h + 1) * 64], pc[:, 512 + h * 64:512 + (h + 1) * 64], sm8[:, 16 + h:17 + h],
                                  ytile[:, h * 64:(h + 1) * 64], ALU.mult, ALU.add, [pc, sm8, ytile], [ytile])
                        k.tt(ytile[:, :], ytile[:, :], gate[:, :], ALU.mult, [ytile, gate], [ytile])
                        for c in range(2):
                            pt_ = k.ps()
                            k.tr(pt_[:, 0:128], ytile[:, c * 128:(c + 1) * 128], ident, [ytile, cst], [pt_])
                            k.evac(yT[:, c, :], pt_[:, 0:128], [pt_], [yT])
                        k.dma(mixT[768:1024, i * 128:(i + 1) * 128].rearrange("(c p) t -> p c t", p=128), yT[:, :, :], [yT], [mixT])

                    for d in (0, 1):
                        for h in range(4):
                            k.memset(HB[h]["H"][:, :], 0.0, [HB[h]["H"]])
                            k.memset(HB[h]["Hb"][:, :], 0.0, [HB[h]["Hb"]], eng="pool")
                        order = list(range(NT)) if d == 0 else list(range(NT - 1, -1, -1))
                        k.run_streams([((lambda i=order[0]: rw_prep(i, PB[0], d)), [k.banks[4], k.banks[5]], None)])
                        for n, i in enumerate(order):
                            bodies = [((lambda i=i, h=h, n=n: rw_head(i, h, PB[n % 3], HB[h], d, n % 2)), [k.banks[h]], bbk[h])
                                      for h in range(4)]
                            if n + 1 < len(order):
                                bodies.append(((lambda j=order[n + 1], n=n: rw_prep(j, PB[(n + 1) % 3], d)), [k.banks[4], k.banks[5]], None))
                            if n >= 1:
                                bodies.append(((lambda j=order[n - 1], n=n: rw_post(j, PB[(n - 1) % 3], d, (n - 1) % 2)), [k.banks[6]], None))
                            k.run_streams(bodies)
                        nl = len(order) - 1
                        k.run_streams([((lambda j=order[-1]: rw_post(j, PB[nl % 3], d, nl % 2)), [k.banks[6]], None)])

            with k.phase():
                wst = [k.sb("wst", [128, 2048]) for i in range(2)]
                WO = k.sb("WO", [128, 8 * 1024], BF16)
                wo = lambda r, c0, cw: WO[:, r * 1024 + c0: r * 1024 + c0 + cw]
                load_w(wst, WO, wo, lambda r, c0, cw: w_out[l, r * 128:(r + 1) * 128, c0:c0 + cw], 8, 1024)
                wcast = [k.sb("wcast", [128, 2048], BF16) for i in range(2)]
                n_ = 0
                for r in range(8):
                    for c0 in range(0, 2 * DFF, 2048):
                        cw = min(2048, 2 * DFF - c0)
                        s_ = wst[n_ % 2]
                        c_ = wcast[n_ % 2]
                        n_ += 1
                        k.dma(s_[:, 0:cw], w_up[l, r * 128:(r + 1) * 128, c0:c0 + cw], [], [s_])
                        k.cp(c_[:, 0:cw], s_[:, 0:cw], [s_], [c_], eng="pool")
                        k.dma(wupb[r * 128:(r + 1) * 128, c0:c0 + cw], c_[:, 0:cw], [c_], [wupb])
                xt = [k.sb("xt", [128, 8, TW]) for i in range(2)]
                mx = [k.sb("mx", [128, 8, TW], BF16) for i in range(2)]
                xn = [k.sb("xn", [128, 8, TW]) for i in range(2)]
                sq = k.sb("sq", [128, 8, TW], BF16)
                hT = k.sb("hT", [128, 8, TW], BF16)
                rstd = k.sb("rstd", [128, TW])
                for w in range(NW):
                    t0 = w * TW
                    x_, m_, xn_ = xt[w % 2], mx[w % 2], xn[w % 2]
                    k.dma(x_[:, :, :], xcur[:, t0:t0 + TW].rearrange("(c p) t -> p c t", p=128), [xcur], [x_])
                    k.dma(m_[:, :, :], mixT[:, t0:t0 + TW].rearrange("(c p) t -> p c t", p=128), [mixT], [m_])
                    for j in range(8):
                        pb = k.ps()
                        for c in range(8):
                            k.mm(pb[:, 0:TW], WO[:, c * 1024 + j * 128: c * 1024 + (j + 1) * 128], m_[:, c, :], [WO, m_], [pb],
                                 start=(c == 0), stop=(c == 7))
                        k.tt(xn_[:, j, :], pb[:, 0:TW], x_[:, j, :], ALU.add, [pb, x_], [xn_])
                    k.dma(xs[:, t0:t0 + TW].rearrange("(c p) t -> p c t", p=128), xn_[:, :, :], [xn_], [xs])
                    rms_to_h(xn_, hT, sq, rstd, pv, 8, TW)
                    k.dma(hpad[:, 1 + t0:1 + t0 + TW].rearrange("(c p) t -> p c t", p=128), hT[:, :, :], [hT], [hpad])
            xcur = xs
            if KSTOP == 3:
                P.flush()
                return nc, k

            with k.phase():
                FW = 510 if T >= 512 else T
                nft = (T + FW - 1) // FW
                wst = [k.sb("wst", [128, 2048]) for i in range(2)]
                wdn = k.sb("wdn", [128, 22 * 1024], BF16)
                wd = lambda r, c0, cw: wdn[:, r * 1024 + c0: r * 1024 + c0 + cw]
                load_w(wst, wdn, wd, lambda r, c0, cw: w_dn[l, r * 128:(r + 1) * 128, c0:c0 + cw], 22, 1024)
                wj = [k.sb("wj", [128, 8, 256], BF16) for i in range(3)]
                hin = [k.sb("hin", [128, 8, 512], BF16) for i in range(2)]
                xr = k.sb("xr", [128, 8, 512])
                actT = k.sb("actT", [128, 22, 512], BF16)
                cg = k.sb("cg", [128, 512])
                cv = k.sb("cv", [128, 512])
                yo = k.sb("yo", [128, 8, 512])
                sq = k.sb("sq", [128, 8, 512], BF16)
                rstd = k.sb("rstd", [128, 512])
                last = (l == L - 1)
                nj = 0
                for f in range(nft):
                    s0 = f * FW
                    wv = min(FW, T - s0)
                    wi = wv + 2
                    h_ = hin[f % 2]
                    k.dma(h_[:, :, 0:wi], hpad[:, s0:s0 + wi].rearrange("(c p) t -> p c t", p=128), [hpad], [h_])
                    k.dma(xr[:, :, 0:wv], xs[:, s0:s0 + wv].rearrange("(c p) t -> p c t", p=128), [xs], [xr])
                    for j in range(22):
                        w_ = wj[nj % 3]
                        nj += 1
                        k.dma(w_[:, :, 0:128], wupb[:, j * 128:(j + 1) * 128].rearrange("(c p) n -> p c n", p=128), [wupb], [w_])
                        k.dma(w_[:, :, 128:256], wupb[:, DFF + j * 128:DFF + (j + 1) * 128].rearrange("(c p) n -> p c n", p=128), [wupb], [w_])
                        for half, dst in ((0, cg), (1, cv)):
                            pb = k.ps()
                            for c in range(8):
                                k.mm(pb[:, 0:wi], w_[:, c, half * 128:(half + 1) * 128], h_[:, c, 0:wi], [w_, h_], [pb],
                                     start=(c == 0), stop=(c == 7))
                            jj = 24 + half * 22 + j
                            k.ts(dst[:, 0:wv], pb[:, 0:wv], pv[:, jj:jj + 1], None, ALU.mult, None, [pb, pv], [dst])
                            k.stt(dst[:, 0:wv], pb[:, 1:1 + wv], pv[:, jj + 44:jj + 45], dst[:, 0:wv], ALU.mult, ALU.add, [pb, pv, dst], [dst])
                            k.stt(dst[:, 0:wv], pb[:, 2:2 + wv], pv[:, jj + 88:jj + 89], dst[:, 0:wv], ALU.mult, ALU.add, [pb, pv, dst], [dst])
                        k.act(cg[:, 0:wv], cg[:, 0:wv], AF.Silu, [cg], [cg])
                        k.tt(actT[:, j, 0:wv], cg[:, 0:wv], cv[:, 0:wv], ALU.mult, [cg, cv], [actT], eng="pool")
                    for j in range(8):
                        pb = k.ps()
                        for c in range(22):
                            k.mm(pb[:, 0:wv], wdn[:, c * 1024 + j * 128: c * 1024 + (j + 1) * 128], actT[:, c, 0:wv], [wdn, actT], [pb],
                                 start=(c == 0), stop=(c == 21))
                        k.tt(yo[:, j, 0:wv], pb[:, 0:wv], xr[:, j, 0:wv], ALU.add, [pb, xr], [yo])
                    if not last:
                        k.dma(xs[:, s0:s0 + wv].rearrange("(c p) t -> p c t", p=128), yo[:, :, 0:wv], [yo], [xs])
                    else:
                        rms_to_h(yo, yo, sq, rstd, pv, 16, wv)
                        k.dma(outT[:, s0:s0 + wv].rearrange("(c p) t -> p c t", p=128), yo[:, :, 0:wv], [yo], [outT])
        P.wait_all("sp", [outT.b])
        P.flush()
    return nc, k


def _consts():
    c = np.zeros((128, NCONST), np.float32)
    p = np.arange(128)[:, None]
    f = np.arange(128)[None, :]
    sc = (p // 64) == (f // 64)
    c[:, CI:CI + 128] = np.eye(128)
    c[:, CO:CO + 128] = 1.0
    A = (p <= f) & sc
    B = (p < f) & sc
    C = (p >= f) & sc
    Dm = (p > f) & sc
    c[:, CA:CA + 128] = A
    c[:, CB:CB + 128] = B
    c[:, CC:CC + 128] = C
    c[:, CD:CD + 128] = Dm
    c[:, CSC:CSC + 128] = sc
    c[:, CNSC:CNSC + 128] = ~sc
    c[:, CM0:CM0 + 128] = B
    c[:, CM0 + 128:CM0 + 256] = A
    c[:, CM1:CM1 + 128] = Dm
    c[:, CM1 + 128:CM1 + 256] = C
    half = 8
    inv_freq = (500000.0 ** (-np.arange(half, dtype=np.float32) / half)).astype(np.float32)
    for base in (0, 64):
        for r in range(16):
            c[base + r, CMISC] = inv_freq[r % 8]
            c[base + r, CMISC + 1] = -1.0 if r < 8 else 1.0
    c[:, CMISC + 2] = 1.0
    c[:, CMISC + 3] = 1024 * 1e-6
    c[:, CMISC + 4] = 1e-6
    c[:, CMISC + 5] = 64 * 1e-6
    c[:, CMISC + 6] = -0.5
    c[:, CMISC + 7] = 64 * 64e-5
    return c


def _amask():
    m = np.zeros((3, 128, 384), np.float32)
    q = np.arange(128)[:, None]
    kk = np.arange(384)[None, :]
    band = np.abs(kk - 128 - q) <= 128
    v0 = band & (kk >= 128)
    v1 = band
    v2 = band & (kk < 256)
    for i, v in enumerate((v0, v1, v2)):
        m[i] = np.where(v, 0.0, -30000.0)
    return m


def _layout(inputs, T, L):
    f32 = np.float32
    g = {k_: np.asarray(v) for k_, v in inputs.items()}
    w_in = g["w_in"][:L].astype(f32)

    def swap_block(blk):
        o = np.zeros_like(blk)
        o[..., 0:8] = blk[..., 8:16]
        o[..., 8:16] = blk[..., 0:8]
        return o

    fm = np.zeros((L, D, 1024), f32)
    for j in range(3):
        qa = w_in[:, :, j * 64:(j + 1) * 64]
        qb_ = w_in[:, :, (j + 3) * 64:(j + 4) * 64]
        fm[:, :, j * 128:j * 128 + 64] = qa
        fm[:, :, j * 128 + 64:(j + 1) * 128] = qb_
        fm[:, :, (4 + j) * 128:(4 + j) * 128 + 64] = swap_block(qa)
        fm[:, :, (4 + j) * 128 + 64:(5 + j) * 128] = swap_block(qb_)
    for g_ in range(2):
        kb_ = w_in[:, :, 384 + g_ * 64:384 + (g_ + 1) * 64]
        fm[:, :, 384 + g_ * 64:384 + (g_ + 1) * 64] = kb_
        fm[:, :, 7 * 128 + g_ * 64:7 * 128 + (g_ + 1) * 64] = swap_block(kb_)
    tm = np.ascontiguousarray(w_in[:, :, 512:N_IN])
    pvec = np.zeros((L, 128, 256), f32)
    prow = np.zeros((L, NPR), f32)
    for l in range(L):
        pvec[l, :, 0:8] = g["norm_mix"][l].reshape(8, 128).T
        pvec[l, :, 8:16] = g["norm_ffn"][l].reshape(8, 128).T
        pvec[l, :, 16:24] = g["norm_final"].reshape(8, 128).T
        fc = g["ffn_conv"][l]
        for a in range(3):
            pvec[l, :, 24 + a * 44:24 + (a + 1) * 44] = fc[a].reshape(44, 128).T
        prow[l, PR_SINK:PR_SINK + 6] = g["attn_sink"][l]
        prow[l, PR_CONV:PR_CONV + 5760] = g["dn_conv"][l].reshape(-1)
        prow[l, PR_ALOG:PR_ALOG + 12] = g["dn_a_log"][l].reshape(-1)
        prow[l, PR_DTB:PR_DTB + 12] = g["dn_dt_bias"][l].reshape(-1)
        prow[l, PR_DNN:PR_DNN + 384] = np.tile(g["dn_norm"][l], 6)
        prow[l, PR_MU0:PR_MU0 + 1024] = g["rw_mu"][l, 0]
        prow[l, PR_MU1:PR_MU1 + 1024] = g["rw_mu"][l, 1]
        prow[l, PR_W0:PR_W0 + 512] = g["rw_w0"][l].reshape(-1)
        prow[l, PR_A0:PR_A0 + 512] = g["rw_a0"][l].reshape(-1)
        prow[l, PR_KK:PR_KK + 256] = g["rw_k_k"][l]
        prow[l, PR_KA:PR_KA + 256] = g["rw_k_a"][l]
        prow[l, PR_RK:PR_RK + 256] = g["rw_r_k"][l].reshape(-1)
        prow[l, PR_LNW:PR_LNW + 256] = g["rw_ln_w"][l]
        prow[l, PR_LNB:PR_LNB + 256] = g["rw_ln_b"][l]
    prow = np.ascontiguousarray(np.broadcast_to(prow[:, None, :], (L, 128, NPR)))
    shared = {
        "w_in_fm": fm, "w_in_tm": tm,
        "w_out": np.ascontiguousarray(g["w_out"][:L].astype(f32)),
        "w_up": np.ascontiguousarray(g["ffn_w_up"][:L].astype(f32)),
        "w_down": np.ascontiguousarray(g["ffn_w_down"][:L].astype(f32)),
        "rw_wup": np.ascontiguousarray(g["rw_w_up"][:L].astype(f32)),
        "rw_aup": np.ascontiguousarray(g["rw_a_up"][:L].astype(f32)),
        "rw_gup": np.ascontiguousarray(g["rw_g_up"][:L].astype(f32)),
        "pvec": pvec, "prow": prow, "consts": _consts(), "amask": _amask(),
    }
    return shared


_CACHE = {}


def run(inputs, T, L, batches, mixers=("att", "dn", "rw")):
    key = (T, L, tuple(mixers))
    if key not in _CACHE:
        _CACHE[key] = build(T, L, mixers)[0]
    nc = _CACHE[key]
    shared = _layout(inputs, T, L)
    x = np.asarray(inputs["x"])
    pos = np.asarray(inputs["positions"])
    in_maps = []
    for b in batches:
        m = dict(shared)
        m["xT"] = np.ascontiguousarray(x[b, :T, :].T.astype(np.float32))
        m["posf"] = np.ascontiguousarray(np.broadcast_to(pos[b, :T].astype(np.float32)[None, :], (16, T)))
        in_maps.append(m)
    import os
    if os.environ.get("KTRACE"):
        res = run_bass_kernel_spmd(nc, in_maps, core_ids=list(range(len(batches))), trace=True)
        print("EXEC_TIME_NS", res.exec_time_ns)
    else:
        res = run_bass_kernel_spmd(nc, in_maps, core_ids=list(range(len(batches))))
    out = np.stack([np.ascontiguousarray(r["outT"].T) for r in res.results], axis=0)
    return out.astype(np.float32)


def kernel(**inputs):
    x = np.asarray(inputs["x"])
    B, T, _ = x.shape
    L = np.asarray(inputs["w_in"]).shape[0]
    return run(inputs, T, L, list(range(B)))
```
